# Optimizing a Trainium2 kernel written in Bass

```python
import jax, jax.numpy as jnp
from jax import lax
import numpy as np

D_MODEL = 1024
BATCH = 8
SEQ = 4096
DEPTH = 4

N_META = 16
GRID_W = 64
GLA_HEADS = 4
GLA_DK = 64
GLA_DV = 128
GLA_KEY = GLA_HEADS * GLA_DK
GLA_VAL = GLA_HEADS * GLA_DV
GLA_RANK = 16
GLA_TAU = 16.0
CHUNK = 64
ATT_HEADS = 8
ATT_KV_HEADS = 2
HEAD_DIM = 64
ATT_Q = ATT_HEADS * HEAD_DIM
ATT_KV = ATT_KV_HEADS * HEAD_DIM
Q_BLOCK = 128
ROPE_THETA = 10000.0
D_FF = 2816
EPS = 1e-6

IN_SIZES = (GLA_KEY, GLA_KEY, GLA_VAL, GLA_VAL, GLA_RANK, GLA_RANK,
            ATT_Q, ATT_KV, ATT_KV, D_MODEL, D_MODEL)
D_IN = sum(IN_SIZES)
SPLIT_POINTS = tuple(int(s) for s in np.cumsum(IN_SIZES)[:-1])

kernel_name = "hybrid_gla_gqa_macaron_encoder"


def rmsnorm(x, g):
    xf = x.astype(jnp.float32)
    y = xf * lax.rsqrt(jnp.mean(xf * xf, axis=-1, keepdims=True) + EPS)
    return (y * g.astype(jnp.float32)).astype(x.dtype)


def swiglu(x, w_gate, w_up, w_down):
    return (jax.nn.silu(x @ w_gate) * (x @ w_up)) @ w_down


def gla_causal_chunked(q, k, v, log_a):
    f32 = jnp.float32
    B, H, T, dk = q.shape
    dv = v.shape[-1]
    n = T // CHUNK
    qc = q.astype(f32).reshape(B, H, n, CHUNK, dk)
    kc = k.astype(f32).reshape(B, H, n, CHUNK, dk)
    vc = v.astype(f32).reshape(B, H, n, CHUNK, dv)
    bcum = jnp.cumsum(log_a.astype(f32).reshape(B, H, n, CHUNK, dk), axis=3)
    btot = bcum[:, :, :, -1:, :]
    q_dec = qc * jnp.exp(bcum)
    k_inv = kc * jnp.exp(-bcum)
    k_end = kc * jnp.exp(btot - bcum)
    mask = jnp.tril(jnp.ones((CHUNK, CHUNK), dtype=bool))
    att = jnp.where(mask, jnp.einsum('bhnid,bhnjd->bhnij', q_dec, k_inv), 0.0)
    o_intra = jnp.einsum('bhnij,bhnjv->bhniv', att, vc)
    kv_chunk = jnp.einsum('bhnjd,bhnjv->bhndv', k_end, vc)
    decay = jnp.exp(btot[:, :, :, 0, :])

    def step(S, inp):
        d, kv = inp
        return d[..., None] * S + kv, S

    S0 = jnp.zeros((B, H, dk, dv), f32)
    _, S_prev = lax.scan(step, S0, (jnp.moveaxis(decay, 2, 0), jnp.moveaxis(kv_chunk, 2, 0)))
    S_prev = jnp.moveaxis(S_prev, 0, 2)
    o_inter = jnp.einsum('bhnid,bhndv->bhniv', q_dec, S_prev)
    return (o_intra + o_inter).reshape(B, H, T, dv)


def gla_branch(q, k, v, r, lr_f, lr_b, w2, b2, gn_gain):
    B, L, _ = q.shape
    pad = CHUNK - N_META

    def heads(t, d):
        return t.reshape(B, L, GLA_HEADS, d).transpose(0, 2, 1, 3)

    def log_gate(lr, w, b):
        return jax.nn.log_sigmoid((lr @ w + b).astype(jnp.float32)) / GLA_TAU

    def padseq(t):
        return jnp.pad(t, ((0, 0), (0, 0), (pad, 0), (0, 0)))

    def flip(t):
        return jnp.flip(t, axis=2)

    qh = padseq(heads(q * GLA_DK ** -0.5, GLA_DK))
    kh = padseq(heads(k, GLA_DK))
    vh = padseq(heads(v, GLA_DV))
    gf = padseq(heads(log_gate(lr_f, w2[0], b2[0]), GLA_DK))
    gb = padseq(heads(log_gate(lr_b, w2[1], b2[1]), GLA_DK))
    o_f = gla_causal_chunked(qh, kh, vh, gf)
    o_b = flip(gla_causal_chunked(flip(qh), flip(kh), flip(vh), flip(gb)))
    o = (o_f + o_b)[:, :, pad:, :].transpose(0, 2, 1, 3)
    o = o * lax.rsqrt(jnp.mean(o * o, axis=-1, keepdims=True) + EPS)
    o = o.reshape(B, L, GLA_VAL) * gn_gain.astype(jnp.float32)
    return o.astype(r.dtype) * jax.nn.silu(r)


def axial_rope_tables(n_tok):
    f32 = jnp.float32
    rows = n_tok // GRID_W
    row = jnp.repeat(jnp.arange(rows), GRID_W).astype(f32)
    col = jnp.tile(jnp.arange(GRID_W), rows).astype(f32)
    axis_dim = HEAD_DIM // 2
    inv = ROPE_THETA ** (-jnp.arange(0, axis_dim, 2, dtype=f32) / axis_dim)
    ang = jnp.concatenate([row[:, None] * inv, col[:, None] * inv], axis=-1)
    ang = jnp.concatenate([jnp.zeros((N_META, axis_dim), f32), ang], axis=0)
    return jnp.cos(ang), jnp.sin(ang)


def apply_axial_rope(x, cos, sin):
    L = x.shape[-2]
    q4 = HEAD_DIM // 4
    xs = x.astype(jnp.float32).reshape(*x.shape[:-1], 2, 2, q4)
    x1, x2 = xs[..., 0, :], xs[..., 1, :]
    c = cos.reshape(L, 2, q4)
    s = sin.reshape(L, 2, q4)
    return jnp.stack([x1 * c - x2 * s, x2 * c + x1 * s], axis=-2).reshape(x.shape)


def attention_branch(q, k, v, gq, gk, cos, sin):
    B, L, _ = q.shape
    G = ATT_HEADS // ATT_KV_HEADS
    qh = q.reshape(B, L, ATT_KV_HEADS, G, HEAD_DIM).transpose(0, 2, 3, 1, 4)
    kh = k.reshape(B, L, ATT_KV_HEADS, HEAD_DIM).transpose(0, 2, 1, 3)
    vh = v.reshape(B, L, ATT_KV_HEADS, HEAD_DIM).transpose(0, 2, 1, 3)
    qh = apply_axial_rope(rmsnorm(qh, gq), cos, sin) * HEAD_DIM ** -0.5
    kh = apply_axial_rope(rmsnorm(kh, gk), cos, sin)
    pad = Q_BLOCK - N_META
    qp = jnp.pad(qh, ((0, 0), (0, 0), (0, 0), (pad, 0), (0, 0)))
    nblk = qp.shape[3] // Q_BLOCK
    qb = jnp.moveaxis(qp.reshape(B, ATT_KV_HEADS, G, nblk, Q_BLOCK, HEAD_DIM), 3, 0)

    def block(qblk):
        s = jnp.einsum('bkgqd,bksd->bkgqs', qblk, kh)
        p = jax.nn.softmax(s, axis=-1)
        return jnp.einsum('bkgqs,bksd->bkgqd', p.astype(vh.dtype), vh)

    o = lax.map(block, qb)
    o = jnp.moveaxis(o, 0, 3).reshape(B, ATT_KV_HEADS, G, nblk * Q_BLOCK, HEAD_DIM)[:, :, :, pad:]
    return o.transpose(0, 3, 1, 2, 4).reshape(B, L, ATT_Q).astype(q.dtype)


def mixer(z, w_in, gla_w2, gla_b2, gla_gn, q_norm, k_norm, w_pa, w_pb, b_merge, w_out, cos, sin):
    h = z @ w_in
    (q_a, k_a, v_a, r_a, lr_f, lr_b, q_b, k_b, v_b, g_a, g_b) = jnp.split(h, SPLIT_POINTS, axis=-1)
    a = gla_branch(q_a, k_a, v_a, r_a, lr_f, lr_b, gla_w2, gla_b2, gla_gn)
    b = attention_branch(q_b, k_b, v_b, q_norm, k_norm, cos, sin)
    y = (jax.nn.sigmoid(g_a + b_merge[0]) * (a @ w_pa)
         + jax.nn.sigmoid(g_b + b_merge[1]) * (b @ w_pb))
    return y @ w_out


def setup_inputs(seed: int = 0) -> dict:
    key = jax.random.key(seed)
    ks = jax.random.split(key, 18)

    def nrm(k, shape, scale):
        return jax.random.normal(k, shape, jnp.float32) * scale

    return {
        "x": nrm(ks[0], (BATCH, SEQ, D_MODEL), 1.0),
        "meta_tokens": nrm(ks[1], (N_META, D_MODEL), 1.0),
        "norm_gains": 1.0 + nrm(ks[2], (DEPTH, 3, D_MODEL), 0.02),
        "ffn_w_gate": nrm(ks[3], (DEPTH, 2, D_MODEL, D_FF), D_MODEL ** -0.5),
        "ffn_w_up": nrm(ks[4], (DEPTH, 2, D_MODEL, D_FF), D_MODEL ** -0.5),
        "ffn_w_down": nrm(ks[5], (DEPTH, 2, D_FF, D_MODEL), D_FF ** -0.5),
        "w_in": nrm(ks[6], (DEPTH, D_MODEL, D_IN), D_MODEL ** -0.5),
        "gla_w2": nrm(ks[7], (DEPTH, 2, GLA_RANK, GLA_KEY), GLA_RANK ** -0.5),
        "gla_b2": nrm(ks[8], (DEPTH, 2, GLA_KEY), 0.1),
        "gla_gn": 1.0 + nrm(ks[9], (DEPTH, GLA_VAL), 0.02),
        "q_norm": 1.0 + nrm(ks[10], (DEPTH, HEAD_DIM), 0.02),
        "k_norm": 1.0 + nrm(ks[11], (DEPTH, HEAD_DIM), 0.02),
        "w_pa": nrm(ks[12], (DEPTH, GLA_VAL, D_MODEL), GLA_VAL ** -0.5),
        "w_pb": nrm(ks[13], (DEPTH, ATT_Q, D_MODEL), ATT_Q ** -0.5),
        "b_merge": nrm(ks[14], (DEPTH, 2, D_MODEL), 0.02),
        "w_out": nrm(ks[15], (DEPTH, D_MODEL, D_MODEL), D_MODEL ** -0.5),
        "final_norm": 1.0 + nrm(ks[16], (D_MODEL,), 0.02),
    }


def reference(x, meta_tokens, norm_gains, ffn_w_gate, ffn_w_up, ffn_w_down, w_in, gla_w2, gla_b2,
              gla_gn, q_norm, k_norm, w_pa, w_pb, b_merge, w_out, final_norm):
    B, N, D = x.shape
    meta = jnp.broadcast_to(meta_tokens[None].astype(x.dtype), (B, N_META, D))
    h = jnp.concatenate([meta, x], axis=1)
    cos, sin = axial_rope_tables(N)
    for l in range(DEPTH):
        h = h + 0.5 * swiglu(rmsnorm(h, norm_gains[l, 0]),
                             ffn_w_gate[l, 0], ffn_w_up[l, 0], ffn_w_down[l, 0])
        h = h + mixer(rmsnorm(h, norm_gains[l, 1]), w_in[l], gla_w2[l], gla_b2[l], gla_gn[l],
                      q_norm[l], k_norm[l], w_pa[l], w_pb[l], b_merge[l], w_out[l], cos, sin)
        h = h + 0.5 * swiglu(rmsnorm(h, norm_gains[l, 2]),
                             ffn_w_gate[l, 1], ffn_w_up[l, 1], ffn_w_down[l, 1])
    return rmsnorm(h, final_norm)[:, N_META:]
```

```python
import contextlib
import numpy as np
import ml_dtypes
import concourse.bass as bass
import concourse.mybir as mybir
from concourse.bass_utils import run_bass_kernel_spmd

F32 = mybir.dt.float32
BF16 = mybir.dt.bfloat16
AF = mybir.ActivationFunctionType
ALU = mybir.AluOpType

D = 1024
DFF = 2816
NFC = DFF // 128
DEPTH = 4
EPS = 1e-6
DIN = 4384
SAME_ENGINE_SYNC = True


class Rec:
    ENGS = ("pe", "act", "dve", "pool", "sp")

    def __init__(self, nc, es):
        self.nc = nc
        self.es = es
        self.ops = []
        self.last_w = {}
        self.readers = {}
        self.chan_R = {}
        self.chan_n = {}
        self.slot_last = {}
        self.last_sk = {}
        self.bar = {}
        self.bar_pending = set()

    def barrier(self):
        self.bar = dict(self.last_sk)
        self.bar_pending = set(self.ENGS)

    def chan(self, name, R):
        self.chan_R[name] = R
        self.chan_n[name] = 0

    def add(self, eng, fn, reads=(), writes=(), chan=None):
        i = len(self.ops)
        deps = set()
        for b in reads:
            w = self.last_w.get(b)
            if w is not None:
                deps.add(w)
        for b in writes:
            w = self.last_w.get(b)
            if w is not None:
                deps.add(w)
            r = self.readers.get(b)
            if r:
                deps.update(r.values())
        sk = None
        if chan is not None:
            n = self.chan_n[chan]
            self.chan_n[chan] = n + 1
            sk = ("c", chan, n % self.chan_R[chan])
            prev = self.slot_last.get(sk)
            if prev is not None:
                deps.add(prev)
            self.slot_last[sk] = i
        else:
            sk = ("e", eng)
        if eng in self.bar_pending:
            self.bar_pending.discard(eng)
            deps.update(self.bar.values())
        for b in reads:
            self.readers.setdefault(b, {})[sk] = i
        for b in writes:
            self.last_w[b] = i
            self.readers[b] = {}
        self.last_sk[sk] = i
        self.ops.append([eng, fn, deps, sk])
        return i

    def finalize(self, block):
        ops = self.ops
        nops = len(ops)
        ordn = [0] * nops
        cnt = {}
        for i, (eng, fn, deps, sk) in enumerate(ops):
            cnt[sk] = cnt.get(sk, 0) + 1
            ordn[i] = cnt[sk]
        known = {e: {} for e in self.ENGS}
        waits = [None] * nops
        signal = [False] * nops
        for i, (eng, fn, deps, sk) in enumerate(ops):
            need = {}
            K = known[eng]
            for d in deps:
                dsk = ops[d][3]
                if dsk[0] == "e" and dsk[1] == eng:
                    if eng == "pe" or not SAME_ENGINE_SYNC:
                        continue
                if K.get(dsk, 0) >= ordn[d]:
                    continue
                if dsk not in need or ordn[need[dsk]] < ordn[d]:
                    need[dsk] = d
            for dsk, d in need.items():
                K[dsk] = ordn[d]
                signal[d] = True
            waits[i] = list(need.values())
        val = [0] * nops
        run = {}
        for i, (eng, fn, deps, sk) in enumerate(ops):
            if sk[0] == "c":
                signal[i] = True
                run[sk] = run.get(sk, 0) + 16
                val[i] = run[sk]
            elif signal[i]:
                run[sk] = run.get(sk, 0) + 1
                val[i] = run[sk]
        sems = {}
        for sk in cnt:
            sems[sk] = self.es.enter_context(self.nc.semaphore("s_" + "_".join(str(x) for x in sk[1:])))
        self.nsems = len(sems)
        self.maxvals = dict(run)
        final_chan = {sk: v for sk, v in run.items() if sk[0] == "c"}
        per_eng = {e: [] for e in self.ENGS}
        for i, (eng, fn, deps, sk) in enumerate(ops):
            per_eng[eng].append(i)
        self.eng_counts = {e: (len(v), sum(len(waits[i]) for i in v)) for e, v in per_eng.items()}

        def emit(engname, e):
            for i in per_eng[engname]:
                _, fn, _, sk = ops[i]
                for d in waits[i]:
                    e.wait_ge(sems[ops[d][3]], val[d])
                inst = fn(e)
                if signal[i]:
                    inst.then_inc(sems[sk], 16 if sk[0] == "c" else 1)
            if engname == "sp":
                for sk, v in final_chan.items():
                    e.wait_ge(sems[sk], v)

        @block.tensor
        def _(e):
            emit("pe", e)

        @block.scalar
        def _(e):
            emit("act", e)

        @block.vector
        def _(e):
            emit("dve", e)

        @block.gpsimd
        def _(e):
            emit("pool", e)

        @block.sync
        def _(e):
            emit("sp", e)


WF_N = 1440
WB1_N = 2592
WB2_N = 2048
WEXT_N = WF_N + WB1_N + WB2_N
TMP0 = 36864


def build_program(NT, stages, debug_h=False):
    P = NT * 128
    SEQ = P - 128
    nc = bass.Bass("TRN2", target_bir_lowering=False)

    def din(name, shape):
        return nc.dram_tensor(name, shape, F32, kind="ExternalInput").ap()

    x_d = din("x", [SEQ, D])
    meta_d = din("meta", [16, D])
    ng_d = din("norm_gains", [DEPTH, 3, D])
    wg_d = din("ffn_w_gate", [DEPTH, 2, D, DFF])
    wu_d = din("ffn_w_up", [DEPTH, 2, D, DFF])
    wd_d = din("ffn_w_down", [DEPTH, 2, DFF, D])
    fn_d = din("final_norm", [1, D])
    ident_d = din("ident", [128, 128])
    wext_d = din("w_ext", [DEPTH, D, WEXT_N])
    wpa_d = din("w_pa", [DEPTH, 512, D])
    wpb_d = din("w_pb_r", [DEPTH, 64, 8, D])
    wout_d = din("w_out", [DEPTH, D, D])
    w2aug_d = din("w2aug", [DEPTH, 33, 2, 256])
    cols_d = din("cols", [DEPTH, 128, 32])
    masks_d = din("masks", [128, 4, 128])
    bo_d = din("bo", [128, 2, 128])
    ropeC_d = din("ropeC", [128, P])
    ropeS_d = din("ropeS", [128, P])
    y_d = nc.dram_tensor("y", [SEQ, D], F32, kind="ExternalOutput").ap()
    H = nc.dram_tensor("Hs", [P, D], F32, kind="Internal").ap()
    ZT = nc.dram_tensor("ZTs", [128, 8, P], BF16, kind="Internal").ap()
    OF = nc.dram_tensor("OFs", [128, 4, P], F32, kind="Internal").ap()
    ATs = nc.dram_tensor("ATs", [128, 4, P], BF16, kind="Internal").ap()
    BTs = nc.dram_tensor("BTs", [64, 8, P], BF16, kind="Internal").ap()
    if debug_h:
        hdbg = nc.dram_tensor("hdbg", [P, D], F32, kind="ExternalOutput").ap()

    es = contextlib.ExitStack()
    with es:
        def sb(name, shape, dt):
            return es.enter_context(nc.sbuf_tensor(name, shape, dt))

        def ps(name, shape, dt):
            return es.enter_context(nc.psum_tensor(name, shape, dt))

        rec = Rec(nc, es)
        rec.chan("w", 8)
        rec.chan("hl", 4)
        rec.chan("st", 4)
        rec.chan("misc", 4)
        A = rec.add

        arena = sb("arena", [128, 66 * 1024], BF16)
        ident = sb("ident_sb", [128, 128], BF16)
        gtile = sb("gtile", [128, D], F32)
        hbuf = [sb(f"hbuf{i}", [128, D], F32) for i in range(4)]
        xn = [sb(f"xn{i}", [128, D], BF16) for i in range(2)]
        junk = sb("junk", [128, D], BF16)
        stat = sb("stat", [128, 16], F32)
        xnT = sb("xnT", [128, 8, 512], BF16)
        shared = sb("shared", [128, 13312], BF16)
        masks = sb("masks_sb", [128, 4, 128], BF16)
        mask4 = sb("mask4", [128, 2, 4, 128], BF16)
        bo = sb("bo_sb", [128, 2, 128], BF16)
        w2aug = sb("w2aug_sb", [33, 2, 256], BF16)
        cols = sb("cols_sb", [128, 32], F32)
        Sst = sb("Sst", [64, 512], F32)
        Sbf = sb("Sbf", [64, 512], BF16)

        hT = shared[:, 0:NFC * 512].rearrange("p (j n) -> p j n", j=NFC)
        sgv = shared[:, NFC * 512:NFC * 512 + 2048].bitcast(F32)
        sg = [sgv[:, 0:512], sgv[:, 512:1024]]
        KT = shared[:, 0:P]
        VA = shared[:, P:P + NT * 256].rearrange("p (t g a d) -> p t g a d", t=NT, g=2, a=2)
        assert P + NT * 256 <= 13312

        ps_tr = ps("ps_tr", [128, 8, 128], BF16)
        psb = [ps(f"psb{i}", [128, 512], F32) for i in range(7)]
        ps_g = [psb[0], psb[1]]
        ps_u = [psb[2], psb[3]]
        ps_o = [psb[4], psb[5]]

        def PK(k):
            return ("psb", k)

        A("pool", lambda e: e.dma_start(out=ident[:], in_=ident_d[:]), writes=["ident"], chan="misc")
        A("pool", lambda e: e.dma_start(out=masks[:], in_=masks_d[:]), writes=["masks"], chan="misc")
        A("pool", lambda e: e.dma_start(out=bo[:], in_=bo_d[:]), writes=["bo"], chan="misc")
        for dr in range(2):
            for h in range(4):
                A("pool", lambda e, dr=dr, h=h: e.dma_start(out=mask4[:, dr, h, :], in_=masks_d[:, 2 * dr, :]),
                  writes=["mask4"], chan="misc")

        def stage_init():
            hb = hbuf[0]
            A("dve", lambda e: e.memset(hb[:], 0.0), writes=[("hbuf", 0)])
            A("sp", lambda e: e.dma_start(out=H[0:112, :], in_=hb[0:112, :]), reads=[("hbuf", 0)],
              writes=[("H", 0)], chan="st")
            A("sp", lambda e: e.dma_start(out=H[112:128, :], in_=meta_d[:]), writes=[("H", 0)], chan="st")
            for t in range(1, NT):
                A("sp", lambda e, t=t: e.dma_start(out=H[t * 128:(t + 1) * 128, :],
                                                    in_=x_d[(t - 1) * 128:t * 128, :]),
                  writes=[("H", t)], chan="st")

        cnt = {"h": 0, "xn": 0, "g": 0, "o": 0, "sg": 0, "pt": 0, "sc": 0}

        def load_gain(src_row_ap):
            A("sp", lambda e: e.dma_start(out=gtile[:], in_=src_row_ap.partition_broadcast(128)),
              writes=["gtile"], chan="misc")

        def next_hbuf():
            k = cnt["h"] % 4
            cnt["h"] += 1
            return hbuf[k], ("hbuf", k)

        def load_h(t):
            hb, hkey = next_hbuf()
            A("sp", lambda e: e.dma_start(out=hb[:], in_=H[t * 128:(t + 1) * 128, :]),
              reads=[("H", t)], writes=[hkey], chan="hl")
            return hb, hkey

        def norm_tile(hb, hkey, out_ap, out_key):
            s0 = stat[:, 0:1]
            s1 = stat[:, 1:2]
            s2 = stat[:, 2:3]
            A("act", lambda e: e.activation(out=junk[:], in_=hb[:], func=AF.Square, accum_out=s0),
              reads=[hkey], writes=["junk", "s0"])
            A("act", lambda e: e.activation(out=s1, in_=s0, func=AF.Ln, scale=1.0 / D, bias=EPS),
              reads=["s0"], writes=["s1"])
            A("act", lambda e: e.activation(out=s2, in_=s1, func=AF.Exp, scale=-0.5),
              reads=["s1"], writes=["s2"])
            A("dve", lambda e: e.scalar_tensor_tensor(out=out_ap, in0=hb[:], scalar=s2, in1=gtile[:],
                                                      op0=ALU.mult, op1=ALU.mult),
              reads=[hkey, "s2", "gtile"], writes=[out_key])

        def norm_transpose(hb, hkey, ti):
            kx = cnt["xn"] % 2
            cnt["xn"] += 1
            xnb = xn[kx]
            norm_tile(hb, hkey, xnb[:], ("xn", kx))
            for kc in range(8):
                A("pe", lambda e, kc=kc: e.transpose(ps_tr[:, kc, :], xnb[:, kc * 128:(kc + 1) * 128], ident[:]),
                  reads=[("xn", kx), "ident"], writes=["ps_tr"])
            A("act", lambda e: e.activation(out=xnT[:, :, ti * 128:(ti + 1) * 128], in_=ps_tr[:], func=AF.Copy),
              reads=["ps_tr"], writes=["xnT"])

        def wload(dst, src, key):
            A("pool", lambda e: e.dma_start(out=dst, in_=src), writes=[key], chan="w")

        def ffn_phase(l, f):
            rec.barrier()
            gi = 0 if f == 0 else 2
            load_gain(ng_d[l, gi:gi + 1, :])
            wgv = wg_d[l, f].rearrange("(kc p) n -> p kc n", p=128)
            wuv = wu_d[l, f].rearrange("(kc p) n -> p kc n", p=128)
            WG = arena[:, 0:22 * 1024].rearrange("p (kc n) -> p kc n", kc=8)
            WU = arena[:, 22 * 1024:44 * 1024].rearrange("p (kc n) -> p kc n", kc=8)
            WD = arena[:, 44 * 1024:66 * 1024].rearrange("p (j n) -> p j n", j=NFC)
            groups = [(0, 512), (512, 512), (1024, 512), (1536, 512), (2048, 512), (2560, 256)]
            for (c0, cw) in groups:
                wload(WG[:, :, c0:c0 + cw], wgv[:, :, c0:c0 + cw], ("wg", c0 // 512))
                wload(WU[:, :, c0:c0 + cw], wuv[:, :, c0:c0 + cw], ("wu", c0 // 512))
            wdv = wd_d[l, f].rearrange("(j p) n -> p j n", p=128)
            for j0 in range(0, NFC, 4):
                j1 = min(NFC, j0 + 4)
                wload(WD[:, j0:j1, :], wdv[:, j0:j1, :], ("wd", j0 // 4))

            for b0 in range(0, NT, 4):
                tiles = list(range(b0, min(NT, b0 + 4)))
                n = len(tiles) * 128
                hbs = {}
                for ti, t in enumerate(tiles):
                    hb, hkey = load_h(t)
                    hbs[t] = (hb, hkey)
                    norm_transpose(hb, hkey, ti)
                for j in range(NFC):
                    kg = cnt["g"] % 2
                    cnt["g"] += 1
                    pg, pu = ps_g[kg], ps_u[kg]
                    for kc in range(8):
                        A("pe", lambda e, pg=pg, kc=kc, j=j, n=n: e.matmul(
                            pg[:, :n], WG[:, kc, j * 128:(j + 1) * 128], xnT[:, kc, :n],
                            start=(kc == 0), stop=(kc == 7)),
                            reads=["xnT", ("wg", j // 4)], writes=[PK(kg)])
                    for kc in range(8):
                        A("pe", lambda e, pu=pu, kc=kc, j=j, n=n: e.matmul(
                            pu[:, :n], WU[:, kc, j * 128:(j + 1) * 128], xnT[:, kc, :n],
                            start=(kc == 0), stop=(kc == 7)),
                            reads=["xnT", ("wu", j // 4)], writes=[PK(2 + kg)])
                    ks = cnt["sg"] % 2
                    cnt["sg"] += 1
                    sgb = sg[ks]
                    A("act", lambda e, sgb=sgb, pg=pg, n=n: e.activation(out=sgb[:, :n], in_=pg[:, :n], func=AF.Silu),
                      reads=[PK(kg)], writes=[("sg", ks)])
                    A("dve", lambda e, sgb=sgb, pu=pu, j=j, n=n: e.tensor_tensor(
                        out=hT[:, j, :n], in0=sgb[:, :n], in1=pu[:, :n], op=ALU.mult),
                        reads=[("sg", ks), PK(2 + kg)], writes=[("hT", j)])
                for ti, t in enumerate(tiles):
                    hb, hkey = hbs[t]
                    for half in range(2):
                        ko = cnt["o"] % 2
                        cnt["o"] += 1
                        po = ps_o[ko]
                        for j in range(NFC):
                            A("pe", lambda e, po=po, j=j, ti=ti, half=half: e.matmul(
                                po[:, :], hT[:, j, ti * 128:(ti + 1) * 128], WD[:, j, half * 512:(half + 1) * 512],
                                start=(j == 0), stop=(j == NFC - 1)),
                                reads=[("hT", j), ("wd", j // 4)], writes=[PK(4 + ko)])
                        A("dve", lambda e, po=po, hb=hb, half=half: e.scalar_tensor_tensor(
                            out=hb[:, half * 512:(half + 1) * 512], in0=po[:, :], scalar=0.5,
                            in1=hb[:, half * 512:(half + 1) * 512], op0=ALU.mult, op1=ALU.add),
                            reads=[PK(4 + ko), hkey], writes=[hkey])
                    A("sp", lambda e, hb=hb, t=t: e.dma_start(out=H[t * 128:(t + 1) * 128, :], in_=hb[:]),
                      reads=[hkey], writes=[("H", t)], chan="st")

        class Carver:
            def __init__(self, start):
                self.o = start

            def bf(self, parts, shape_free):
                n = int(np.prod(shape_free))
                n16 = (n + 15) // 16 * 16
                v = arena[0:parts, self.o:self.o + n]
                self.o += n16
                assert self.o <= 66 * 1024, self.o
                return v

            def f32(self, parts, shape_free):
                n = int(np.prod(shape_free)) * 2
                n16 = (n + 15) // 16 * 16
                v = arena[0:parts, self.o:self.o + n].bitcast(F32)
                self.o += n16
                assert self.o <= 66 * 1024, self.o
                return v

        def mm_proj(out_ap, W, c0, M, zsl, key_w, pkey, extra_reads=()):
            for kc in range(8):
                A("pe", lambda e, kc=kc: e.matmul(out_ap, W[:, kc, c0:c0 + M], xnT[:, kc, zsl],
                                                   start=(kc == 0), stop=(kc == 7)),
                  reads=["xnT", key_w] + list(extra_reads), writes=[pkey])

        def mixer_phase(l):
            NB = 2
            rec.barrier()
            load_gain(ng_d[l, 1:2, :])
            A("sp", lambda e: e.dma_start(out=cols[:], in_=cols_d[l]), writes=["cols"], chan="misc")
            A("pool", lambda e: e.dma_start(out=w2aug[:], in_=w2aug_d[l]), writes=["w2aug"], chan="misc")
            wv = wext_d[l].rearrange("(kc p) n -> p kc n", p=128)
            WF = arena[:, 0:8 * WF_N].rearrange("p (kc n) -> p kc n", kc=8)
            wload(WF[:, :, 0:528], wv[:, :, 0:528], "WFa")
            wload(WF[:, :, 528:1056], wv[:, :, 528:1056], "WFa2")
            wload(WF[:, :, 1056:1440], wv[:, :, 1056:1440], "WFb")
            cv = Carver(TMP0)
            T = {}
            T["qTa"] = cv.bf(64, [512]); T["kTa"] = cv.bf(64, [512]); T["lrT"] = cv.bf(33, [128])
            T["la"] = cv.bf(128, [256]); T["kend"] = cv.bf(128, [256]); T["va"] = cv.bf(128, [512])
            T["qdec"] = cv.bf(64, [512]); T["kinv"] = cv.bf(64, [512]); T["attT"] = cv.bf(128, [512])
            T["sqb"] = cv.bf(128, [256])
            T["aT"] = cv.bf(128, [4 * 256]).rearrange("p (h n) -> p h n", h=4)
            T["BT"] = cv.bf(64, [8 * 256]).rearrange("p (h n) -> p h n", h=8)
            T["QT"] = cv.bf(128, [4 * 256]).rearrange("p (j n) -> p j n", j=4)
            T["PT"] = [cv.bf(128, [512]) for _ in range(3)]
            T["e1"] = cv.f32(128, [256]); T["ed"] = cv.f32(64, [512]); T["ei"] = cv.f32(64, [512])
            T["er"] = cv.f32(128, [256])
            T["ob"] = cv.f32(128, [4 * 256]).rearrange("p (h n) -> p h n", h=4)
            T["oft"] = cv.f32(128, [512])
            T["t1"] = cv.f32(128, [256]); T["t2"] = cv.f32(128, [256])
            T["lnt"] = cv.f32(128, [256]); T["rst"] = cv.f32(128, [256])
            T["rc"] = cv.f32(64, [512]); T["sr"] = cv.f32(128, [256])
            T["rC"] = cv.f32(128, [256]); T["rS"] = cv.f32(128, [256])

            A("dve", lambda e: e.memset(T["lrT"][32:33, :], 1.0), writes=["lrT"])
            A("dve", lambda e: e.memset(VA[:, :, :, 1, :], 1.0), writes=["VA"])
            A("dve", lambda e: e.memset(VA[0:112, 0, :, 1, :], 0.0), writes=["VA"])

            def rope_norm(psq, kq, psqs, kqs, pss, kss, n, g0, g1, out_ap, out_key):
                A("act", lambda e: e.activation(out=T["sqb"][:, :n], in_=psq[:, :n], func=AF.Square),
                  reads=[kq], writes=["sqb"])
                A("pe", lambda e: e.matmul(pss[:, :n], bo[:, 0, :], T["sqb"][:, :n], start=True, stop=True),
                  reads=["sqb", "bo"], writes=[kss])
                A("act", lambda e: e.activation(out=T["lnt"][:, :n], in_=pss[:, :n], func=AF.Ln, scale=1.0 / 64, bias=EPS),
                  reads=[kss], writes=["lnt"])
                A("act", lambda e: e.activation(out=T["rst"][:, :n], in_=T["lnt"][:, :n], func=AF.Exp, scale=-0.5),
                  reads=["lnt"], writes=["rst"])
                A("dve", lambda e: e.scalar_tensor_tensor(out=T["t1"][:, :n], in0=psq[:, :n], scalar=g0, in1=T["rC"][:, :n],
                                                          op0=ALU.mult, op1=ALU.mult),
                  reads=[kq, "rope", "cols"], writes=["t1"])
                A("dve", lambda e: e.scalar_tensor_tensor(out=T["t2"][:, :n], in0=psqs[:, :n], scalar=g1, in1=T["rS"][:, :n],
                                                          op0=ALU.mult, op1=ALU.mult),
                  reads=[kqs, "rope", "cols"], writes=["t2"])
                A("dve", lambda e: e.tensor_tensor(out=T["t1"][:, :n], in0=T["t1"][:, :n], in1=T["t2"][:, :n], op=ALU.add),
                  reads=["t1", "t2"], writes=["t1"])
                A("dve", lambda e: e.tensor_tensor(out=out_ap, in0=T["t1"][:, :n], in1=T["rst"][:, :n], op=ALU.mult),
                  reads=["t1", "rst"], writes=[out_key])

            def load_rope(p0, n):
                A("sp", lambda e: e.dma_start(out=T["rC"][:, :n], in_=ropeC_d[:, p0:p0 + n]), writes=["rope"], chan="misc")
                A("sp", lambda e: e.dma_start(out=T["rS"][:, :n], in_=ropeS_d[:, p0:p0 + n]), writes=["rope"], chan="misc")

            def gla_tile(dr, ti, W, wkeys):
                zs = slice(ti * 128, (ti + 1) * 128)
                b0, b1, b2, b3, b4, b5, b6 = psb
                for h in range(4):
                    mm_proj(b0[0:64, h * 128:(h + 1) * 128], W, h * 64, 64, zs, wkeys[0], PK(0))
                for h in range(4):
                    mm_proj(b1[0:64, h * 128:(h + 1) * 128], W, 256 + h * 64, 64, zs, wkeys[0], PK(1))
                mm_proj(b2[0:32, 0:128], W, 1024, 32, zs, wkeys[1], PK(2))
                for kc in range(8):
                    A("pe", lambda e, kc=kc: e.matmul(b3[:, 0:512], xnT[:, kc, zs], W[:, kc, 256:768],
                                                       start=(kc == 0), stop=(kc == 7)),
                      reads=["xnT", wkeys[0], wkeys[1]], writes=[PK(3)])
                for kc in range(8):
                    A("pe", lambda e, kc=kc: e.matmul(b4[:, 0:256], xnT[:, kc, zs], W[:, kc, 768:1024],
                                                       start=(kc == 0), stop=(kc == 7)),
                      reads=["xnT", wkeys[1]], writes=[PK(4)])
                A("act", lambda e: e.activation(out=T["qTa"][:, :], in_=b0[0:64, :], func=AF.Copy, scale=0.125),
                  reads=[PK(0)], writes=["qTa"])
                A("act", lambda e: e.activation(out=T["kTa"][:, :], in_=b1[0:64, :], func=AF.Copy),
                  reads=[PK(1)], writes=["kTa"])
                A("act", lambda e: e.activation(out=T["lrT"][0:32, :], in_=b2[0:32, 0:128], func=AF.Copy),
                  reads=[PK(2)], writes=["lrT"])
                A("dve", lambda e: e.tensor_copy(out=T["va"][:, 0:256], in_=b3[:, 256:512]), reads=[PK(3)], writes=["va"])
                A("dve", lambda e: e.tensor_copy(out=T["va"][:, 256:512], in_=b4[:, 0:256]), reads=[PK(4)], writes=["va"])
                A("pe", lambda e: e.matmul(b2[:, 256:512], T["lrT"][0:33, :], w2aug[0:33, dr, :], start=True, stop=True),
                  reads=["lrT", "w2aug"], writes=[PK(2)])
                A("act", lambda e: e.activation(out=T["e1"][:, :], in_=b2[:, 256:512], func=AF.Exp, scale=-1.0),
                  reads=[PK(2)], writes=["e1"])
                A("act", lambda e: e.activation(out=T["la"][:, :], in_=T["e1"][:, :], func=AF.Ln, bias=1.0),
                  reads=["e1"], writes=["la"])
                for h in range(4):
                    A("pe", lambda e, h=h: e.matmul(b5[0:64, h * 128:(h + 1) * 128], T["la"][:, h * 64:(h + 1) * 64],
                                                     masks[:, 2 * dr, :], start=True, stop=True),
                      reads=["la", "masks"], writes=[PK(5)])
                A("pe", lambda e: e.matmul(b4[:, 256:512], masks[:, 2 * dr + 1, :], T["la"][:, :], start=True, stop=True),
                  reads=["la", "masks"], writes=[PK(4)])
                A("act", lambda e: e.activation(out=T["ed"][:, :], in_=b5[0:64, :], func=AF.Exp, scale=-1.0 / 16),
                  reads=[PK(5)], writes=["ed"])
                A("act", lambda e: e.activation(out=T["ei"][:, :], in_=b5[0:64, :], func=AF.Exp, scale=1.0 / 16),
                  reads=[PK(5)], writes=["ei"])
                A("act", lambda e: e.activation(out=T["er"][:, :], in_=b4[:, 256:512], func=AF.Exp, scale=-1.0 / 16),
                  reads=[PK(4)], writes=["er"])
                A("dve", lambda e: e.tensor_tensor(out=T["qdec"][:, :], in0=T["qTa"][:, :], in1=T["ed"][:, :], op=ALU.mult),
                  reads=["qTa", "ed"], writes=["qdec"])
                A("dve", lambda e: e.tensor_tensor(out=T["kinv"][:, :], in0=T["kTa"][:, :], in1=T["ei"][:, :], op=ALU.mult),
                  reads=["kTa", "ei"], writes=["kinv"])
                A("dve", lambda e: e.tensor_tensor(out=T["kend"][:, :], in0=b3[:, 0:256], in1=T["er"][:, :], op=ALU.mult),
                  reads=[PK(3), "er"], writes=["kend"])
                for h in range(4):
                    A("pe", lambda e, h=h: e.matmul(b6[:, h * 128:(h + 1) * 128], T["kinv"][:, h * 128:(h + 1) * 128],
                                                     T["qdec"][:, h * 128:(h + 1) * 128], start=True, stop=True),
                      reads=["kinv", "qdec"], writes=[PK(6)])
                A("dve", lambda e: e.tensor_tensor(out=T["attT"][:, :], in0=b6[:, :],
                                                   in1=mask4[:, dr].rearrange("p h i -> p (h i)"), op=ALU.mult),
                  reads=[PK(6), "mask4"], writes=["attT"])
                order = (0, 1) if dr == 0 else (1, 0)
                edv = T["ed"].rearrange("p (h i) -> p h i", h=4)
                Sv = Sst[:, :].rearrange("p (h v) -> p h v", h=4)
                for ch in order:
                    cs = slice(ch * 64, (ch + 1) * 64)
                    for h in range(4):
                        osl = slice(h * 128 + ch * 64, h * 128 + ch * 64 + 64)
                        A("pe", lambda e, h=h, osl=osl: e.matmul(b0[:, osl], T["va"][:, h * 128:(h + 1) * 128],
                                                                   T["attT"][:, osl], start=True, stop=False),
                          reads=["va", "attT"], writes=[PK(0)])
                        A("pe", lambda e, h=h, osl=osl: e.matmul(b0[:, osl], Sbf[:, h * 128:(h + 1) * 128],
                                                                   T["qdec"][:, osl], start=False, stop=True),
                          reads=["Sbf", "qdec"], writes=[PK(0)])
                    for h in range(4):
                        A("pe", lambda e, h=h, cs=cs: e.matmul(b1[0:64, h * 128:(h + 1) * 128],
                                                                 T["kend"][cs, h * 64:(h + 1) * 64],
                                                                 T["va"][cs, h * 128:(h + 1) * 128], start=True, stop=True),
                          reads=["kend", "va"], writes=[PK(1)])
                    col = ch * 64 + 63 if dr == 0 else ch * 64
                    A("dve", lambda e, col=col: e.tensor_tensor(out=Sv, in0=Sv,
                                                                 in1=edv[:, :, col:col + 1].to_broadcast([64, 4, 128]),
                                                                 op=ALU.mult),
                      reads=["S", "ed"], writes=["S"])
                    A("dve", lambda e: e.tensor_tensor(out=Sst[:, :], in0=Sst[:, :], in1=b1[0:64, :], op=ALU.add),
                      reads=["S", PK(1)], writes=["S"])
                    A("act", lambda e: e.activation(out=Sbf[:, :], in_=Sst[:, :], func=AF.Copy),
                      reads=["S"], writes=["Sbf"])

            def reset_state():
                A("dve", lambda e: e.memset(Sst[:, :], 0.0), writes=["S"])
                A("dve", lambda e: e.memset(Sbf[:, :], 0.0), writes=["Sbf"])

            reset_state()
            for b0_ in range(0, NT, NB):
                tiles = list(range(b0_, min(NT, b0_ + NB)))
                n = len(tiles) * 128
                p0 = b0_ * 128
                for ti, t in enumerate(tiles):
                    hb, hkey = load_h(t)
                    norm_transpose(hb, hkey, ti)
                A("sp", lambda e, n=n, p0=p0: e.dma_start(out=ZT[:, :, p0:p0 + n], in_=xnT[:, :, :n]),
                  reads=["xnT"], writes=[("ZT", b0_)], chan="st")
                load_rope(p0, n)
                mm_proj(psb[3][:, :n], WF, 1056, 128, slice(0, n), "WFb", PK(3))
                mm_proj(psb[4][:, :n], WF, 1184, 128, slice(0, n), "WFb", PK(4))
                rope_norm(psb[3], PK(3), psb[4], PK(4), psb[5], PK(5), n, cols[:, 2:3], cols[:, 3:4],
                          KT[:, p0:p0 + n], "KT")
                for ti, t in enumerate(tiles):
                    zs = slice(ti * 128, (ti + 1) * 128)
                    for kc in range(8):
                        A("pe", lambda e, kc=kc, zs=zs: e.matmul(psb[6][:, 0:128], xnT[:, kc, zs], WF[:, kc, 1312:1440],
                                                                   start=(kc == 0), stop=(kc == 7)),
                          reads=["xnT", "WFb"], writes=[PK(6)])
                    A("act", lambda e, t=t: e.activation(out=VA[:, t, :, 0, :],
                                                         in_=psb[6][:, 0:128].rearrange("p (g d) -> p g d", g=2),
                                                         func=AF.Copy),
                      reads=[PK(6)], writes=["VA"])
                for ti, t in enumerate(tiles):
                    gla_tile(0, ti, WF, ("WFa", "WFa2"))
                    A("act", lambda e: e.activation(out=T["oft"][:, :], in_=psb[0][:, :], func=AF.Copy),
                      reads=[PK(0)], writes=["oft"])
                    A("sp", lambda e, t=t: e.dma_start(out=OF[:, :, t * 128:(t + 1) * 128],
                                                        in_=T["oft"].rearrange("p (h i) -> p h i", h=4)),
                      reads=["oft"], writes=[("OF", t)], chan="st")

            rec.barrier()
            WB1 = arena[:, 0:8 * WB1_N].rearrange("p (kc n) -> p kc n", kc=8)
            o1 = WF_N
            wload(WB1[:, :, 0:528], wv[:, :, o1:o1 + 528], "WBa")
            wload(WB1[:, :, 528:1056], wv[:, :, o1 + 528:o1 + 1056], "WBa2")
            wload(WB1[:, :, 1056:1568], wv[:, :, o1 + 1056:o1 + 1568], "WBr")
            wload(WB1[:, :, 1568:2080], wv[:, :, o1 + 1568:o1 + 2080], "WBq")
            wload(WB1[:, :, 2080:2592], wv[:, :, o1 + 2080:o1 + 2592], "WBqs")
            reset_state()
            blocks = list(range(0, NT, NB))
            for b0_ in reversed(blocks):
                tiles = list(range(b0_, min(NT, b0_ + NB)))
                n = len(tiles) * 128
                p0 = b0_ * 128
                A("sp", lambda e, n=n, p0=p0: e.dma_start(out=xnT[:, :, :n], in_=ZT[:, :, p0:p0 + n]),
                  reads=[("ZT", b0_)], writes=["xnT"], chan="hl")
                load_rope(p0, n)
                for ti in reversed(range(len(tiles))):
                    t = tiles[ti]
                    A("sp", lambda e, t=t: e.dma_start(out=T["oft"].rearrange("p (h i) -> p h i", h=4),
                                                        in_=OF[:, :, t * 128:(t + 1) * 128]),
                      reads=[("OF", t)], writes=["oft"], chan="hl")
                    gla_tile(1, ti, WB1, ("WBa", "WBa2"))
                    A("dve", lambda e, ti=ti: e.tensor_tensor(out=T["ob"][:, :, ti * 128:(ti + 1) * 128],
                                                               in0=psb[0][:, :].rearrange("p (h i) -> p h i", h=4),
                                                               in1=T["oft"].rearrange("p (h i) -> p h i", h=4), op=ALU.add),
                      reads=[PK(0), "oft"], writes=["ob"])
                for h in range(4):
                    A("act", lambda e, h=h, n=n: e.activation(out=T["sqb"][:, :n], in_=T["ob"][:, h, :n], func=AF.Square),
                      reads=["ob"], writes=["sqb"])
                    A("pe", lambda e, n=n: e.matmul(psb[3][:, :n], bo[:, 1, :], T["sqb"][:, :n], start=True, stop=True),
                      reads=["sqb", "bo"], writes=[PK(3)])
                    A("act", lambda e, n=n: e.activation(out=T["lnt"][:, :n], in_=psb[3][:, :n], func=AF.Ln,
                                                         scale=1.0 / 128, bias=EPS),
                      reads=[PK(3)], writes=["lnt"])
                    A("act", lambda e, n=n: e.activation(out=T["rst"][:, :n], in_=T["lnt"][:, :n], func=AF.Exp, scale=-0.5),
                      reads=["lnt"], writes=["rst"])
                    mm_proj(psb[4][:, :n], WB1, 1056 + h * 128, 128, slice(0, n), "WBr", PK(4))
                    A("act", lambda e, n=n: e.activation(out=T["sr"][:, :n], in_=psb[4][:, :n], func=AF.Silu),
                      reads=[PK(4)], writes=["sr"])
                    A("dve", lambda e, h=h, n=n: e.scalar_tensor_tensor(out=T["t1"][:, :n], in0=T["ob"][:, h, :n],
                                                                         scalar=cols[:, 4 + h:5 + h], in1=T["rst"][:, :n],
                                                                         op0=ALU.mult, op1=ALU.mult),
                      reads=["ob", "rst", "cols"], writes=["t1"])
                    A("dve", lambda e, h=h, n=n: e.tensor_tensor(out=T["aT"][:, h, :n], in0=T["t1"][:, :n],
                                                                  in1=T["sr"][:, :n], op=ALU.mult),
                      reads=["t1", "sr"], writes=["aT"])
                A("sp", lambda e, n=n, p0=p0: e.dma_start(out=ATs[:, :, p0:p0 + n], in_=T["aT"][:, :, :n]),
                  reads=["aT"], writes=[("ATs", b0_)], chan="st")
                for j in range(4):
                    mm_proj(psb[3][:, :n], WB1, 1568 + j * 128, 128, slice(0, n), "WBq", PK(3))
                    mm_proj(psb[4][:, :n], WB1, 2080 + j * 128, 128, slice(0, n), "WBqs", PK(4))
                    rope_norm(psb[3], PK(3), psb[4], PK(4), psb[5], PK(5), n, cols[:, 0:1], cols[:, 1:2],
                              T["QT"][:, j, :n], "QT")
                for ti, t in enumerate(tiles):
                    qs = slice(ti * 128, (ti + 1) * 128)
                    for sblk in range(NT):
                        for g in range(2):
                            ksc = cnt["sc"] % 4
                            cnt["sc"] += 1
                            sc = psb[ksc]
                            A("pe", lambda e, sc=sc, g=g, sblk=sblk, qs=qs: e.matmul(
                                sc[:, :], KT[g * 64:(g + 1) * 64, sblk * 128:(sblk + 1) * 128],
                                T["QT"][g * 64:(g + 1) * 64, :, qs], start=True, stop=True),
                                reads=["KT", "QT"], writes=[PK(ksc)])
                            kp = cnt["pt"] % 3
                            cnt["pt"] += 1
                            pt = T["PT"][kp]
                            A("act", lambda e, sc=sc, pt=pt: e.activation(out=pt[:, :], in_=sc[:, :], func=AF.Exp, scale=0.125),
                              reads=[PK(ksc)], writes=[("PT", kp)])
                            A("pe", lambda e, pt=pt, g=g, sblk=sblk: e.matmul(
                                psb[4 + g][:, :], VA[:, sblk, g, :, :].rearrange("p a d -> p (a d)"), pt[:, :],
                                start=(sblk == 0), stop=(sblk == NT - 1)),
                                reads=[("PT", kp), "VA"], writes=[PK(4 + g)])
                    for g in range(2):
                        A("dve", lambda e, g=g: e.reciprocal(out=T["rc"][:, :], in_=psb[4 + g][64:128, :]),
                          reads=[PK(4 + g)], writes=["rc"])
                        A("dve", lambda e, g=g, qs=qs: e.tensor_tensor(
                            out=T["BT"][:, g * 4:(g + 1) * 4, qs],
                            in0=psb[4 + g][0:64, :].rearrange("p (j q) -> p j q", j=4),
                            in1=T["rc"][:, :].rearrange("p (j q) -> p j q", j=4), op=ALU.mult),
                            reads=[PK(4 + g), "rc"], writes=["BT"])
                A("sp", lambda e, n=n, p0=p0: e.dma_start(out=BTs[:, :, p0:p0 + n], in_=T["BT"][:, :, :n]),
                  reads=["BT"], writes=[("BTs", b0_)], chan="st")

            rec.barrier()
            o2 = WF_N + WB1_N
            WG2 = arena[:, 0:16384].rearrange("p (kc n) -> p kc n", kc=8)
            WPA = arena[:, 16384:20480].rearrange("p (kc n) -> p kc n", kc=4)
            WPB = arena[0:64, 20480:28672].rearrange("p (h n) -> p h n", h=8)
            WOUT = arena[:, 28672:36864].rearrange("p (kc n) -> p kc n", kc=8)
            wload(WG2[:, :, 0:1024], wv[:, :, o2:o2 + 1024], "WGa")
            wload(WG2[:, :, 1024:2048], wv[:, :, o2 + 1024:o2 + 2048], "WGb")
            wload(WPA, wpa_d[l].rearrange("(kc p) n -> p kc n", p=128), "WPA")
            wload(WPB, wpb_d[l], "WPB")
            wload(WOUT, wout_d[l].rearrange("(kc p) n -> p kc n", p=128), "WOUT")
            cv2 = Carver(TMP0)
            aTl = cv2.bf(128, [4 * 512]).rearrange("p (h n) -> p h n", h=4)
            BTl = cv2.bf(64, [8 * 512]).rearrange("p (h n) -> p h n", h=8)
            yT = cv2.bf(128, [8 * 512]).rearrange("p (c n) -> p c n", c=8)
            sga = cv2.f32(128, [512]); sgb_ = cv2.f32(128, [512])
            u1 = cv2.f32(128, [512]); u2 = cv2.f32(128, [512])
            for b0_ in range(0, NT, 4):
                tiles = list(range(b0_, min(NT, b0_ + 4)))
                n = len(tiles) * 128
                p0 = b0_ * 128
                A("sp", lambda e, n=n, p0=p0: e.dma_start(out=xnT[:, :, :n], in_=ZT[:, :, p0:p0 + n]),
                  writes=["xnT"], chan="hl")
                A("sp", lambda e, n=n, p0=p0: e.dma_start(out=aTl[:, :, :n], in_=ATs[:, :, p0:p0 + n]),
                  writes=["aTl"], chan="hl")
                A("sp", lambda e, n=n, p0=p0: e.dma_start(out=BTl[:, :, :n], in_=BTs[:, :, p0:p0 + n]),
                  writes=["BTl"], chan="hl")
                for c in range(8):
                    mm_proj(psb[0][:, :n], WG2, c * 128, 128, slice(0, n), "WGa", PK(0))
                    mm_proj(psb[1][:, :n], WG2, 1024 + c * 128, 128, slice(0, n), "WGb", PK(1))
                    A("act", lambda e, c=c, n=n: e.activation(out=sga[:, :n], in_=psb[0][:, :n], func=AF.Sigmoid,
                                                              bias=cols[:, 8 + c:9 + c]),
                      reads=[PK(0), "cols"], writes=["sga"])
                    A("act", lambda e, c=c, n=n: e.activation(out=sgb_[:, :n], in_=psb[1][:, :n], func=AF.Sigmoid,
                                                              bias=cols[:, 16 + c:17 + c]),
                      reads=[PK(1), "cols"], writes=["sgb"])
                    for kc in range(4):
                        A("pe", lambda e, kc=kc, c=c, n=n: e.matmul(psb[2][:, :n], WPA[:, kc, c * 128:(c + 1) * 128],
                                                                     aTl[:, kc, :n], start=(kc == 0), stop=(kc == 3)),
                          reads=["aTl", "WPA"], writes=[PK(2)])
                    for h in range(8):
                        A("pe", lambda e, h=h, c=c, n=n: e.matmul(psb[3][:, :n], WPB[:, h, c * 128:(c + 1) * 128],
                                                                   BTl[:, h, :n], start=(h == 0), stop=(h == 7)),
                          reads=["BTl", "WPB"], writes=[PK(3)])
                    A("dve", lambda e, n=n: e.tensor_tensor(out=u1[:, :n], in0=sga[:, :n], in1=psb[2][:, :n], op=ALU.mult),
                      reads=["sga", PK(2)], writes=["u1"])
                    A("dve", lambda e, n=n: e.tensor_tensor(out=u2[:, :n], in0=sgb_[:, :n], in1=psb[3][:, :n], op=ALU.mult),
                      reads=["sgb", PK(3)], writes=["u2"])
                    A("dve", lambda e, c=c, n=n: e.tensor_tensor(out=yT[:, c, :n], in0=u1[:, :n], in1=u2[:, :n], op=ALU.add),
                      reads=["u1", "u2"], writes=["yT"])
                for ti, t in enumerate(tiles):
                    hb, hkey = load_h(t)
                    for half in range(2):
                        ko = cnt["o"] % 2
                        cnt["o"] += 1
                        po = psb[4 + ko]
                        for kc in range(8):
                            A("pe", lambda e, po=po, kc=kc, ti=ti, half=half: e.matmul(
                                po[:, :], yT[:, kc, ti * 128:(ti + 1) * 128], WOUT[:, kc, half * 512:(half + 1) * 512],
                                start=(kc == 0), stop=(kc == 7)),
                                reads=["yT", "WOUT"], writes=[PK(4 + ko)])
                        A("dve", lambda e, po=po, hb=hb, half=half: e.tensor_tensor(
                            out=hb[:, half * 512:(half + 1) * 512], in0=po[:, :],
                            in1=hb[:, half * 512:(half + 1) * 512], op=ALU.add),
                            reads=[PK(4 + ko), hkey], writes=[hkey])
                    if t == 0:
                        A("dve", lambda e, hb=hb: e.tensor_scalar(out=hb[:], in0=hb[:], scalar1=cols[:, 24:25],
                                                                   scalar2=None, op0=ALU.mult),
                          reads=[hkey, "cols"], writes=[hkey])
                    A("sp", lambda e, hb=hb, t=t: e.dma_start(out=H[t * 128:(t + 1) * 128, :], in_=hb[:]),
                      reads=[hkey], writes=[("H", t)], chan="st")

        def final_phase():
            rec.barrier()
            load_gain(fn_d[0:1, :])
            for t in range(1, NT):
                hb, hkey = load_h(t)
                norm_tile(hb, hkey, hb[:], hkey)
                A("sp", lambda e, hb=hb, t=t: e.dma_start(out=y_d[(t - 1) * 128:t * 128, :], in_=hb[:]),
                  reads=[hkey], writes=[("Y", t)], chan="st")

        def dump_h():
            rec.barrier()
            for t in range(NT):
                hb, hkey = load_h(t)
                A("sp", lambda e, hb=hb, t=t: e.dma_start(out=hdbg[t * 128:(t + 1) * 128, :], in_=hb[:]),
                  reads=[hkey], writes=[("HD", t)], chan="st")

        for st in stages:
            if st[0] == "init":
                stage_init()
            elif st[0] == "ffn":
                ffn_phase(st[1], st[2])
            elif st[0] == "mixer":
                mixer_phase(st[1])
            elif st[0] == "final":
                final_phase()
        import os as _os
        for _i in range(int(_os.environ.get("DUMMY_ACT", "0"))):
            A("act", lambda e: e.activation(out=junk[:, 0:64], in_=junk[:, 64:128], func=AF.Copy), writes=["junkx"])
        for _i in range(int(_os.environ.get("DUMMY_DVE", "0"))):
            A("dve", lambda e: e.tensor_copy(out=junk[:, 128:192], in_=junk[:, 192:256]), writes=["junky"])
        for _i in range(int(_os.environ.get("DUMMY_SP", "0"))):
            A("sp", lambda e: e.dma_start(out=hbuf[0][:], in_=H[0:128, :]), writes=[("hbuf", 0)], chan="hl")
        if debug_h:
            dump_h()

        block = es.enter_context(nc.Block())
        rec.finalize(block)
        print("ops", len(rec.ops), "sems", rec.nsems, "maxsem", max(rec.maxvals.values()), rec.eng_counts, flush=True)
    return nc


def all_stages():
    st = [("init",)]
    for l in range(DEPTH):
        st.append(("ffn", l, 0))
        st.append(("mixer", l))
        st.append(("ffn", l, 1))
    st.append(("final",))
    return st


def _swap_idx64():
    d = np.arange(64)
    a, b, i = d // 32, (d % 32) // 16, d % 16
    return a * 32 + (1 - b) * 16 + i


def host_consts(NT):
    P = NT * 128
    c = {}
    c["ident"] = np.eye(128, dtype=np.float32)
    j = np.arange(128)[:, None]
    i = np.arange(128)[None, :]
    same = (j // 64) == (i // 64)
    m = np.zeros((128, 4, 128), np.float32)
    m[:, 0] = (same & (j <= i))
    m[:, 1] = (same & (j > i))
    m[:, 2] = (same & (j >= i))
    m[:, 3] = (same & (j < i))
    c["masks"] = m
    bo = np.zeros((128, 2, 128), np.float32)
    bo[:, 0] = same
    bo[:, 1] = 1.0
    c["bo"] = bo
    pos = np.arange(P)
    n = np.maximum(pos - 128, 0)
    row = (n // 64).astype(np.float32)
    colp = (n % 64).astype(np.float32)
    inv = (np.float32(10000.0) ** (-(np.arange(0, 32, 2, dtype=np.float32)) / np.float32(32))).astype(np.float32)
    ang = np.stack([row[:, None] * inv[None, :], colp[:, None] * inv[None, :]], axis=1).astype(np.float32)
    ang[pos < 128] = 0.0
    cs = np.cos(ang).astype(np.float32)
    sn = np.sin(ang).astype(np.float32)
    d = np.arange(64)
    a, b, ii = d // 32, (d % 32) // 16, d % 16
    C64 = cs[:, a, ii].T
    S64 = (sn[:, a, ii] * np.where(b == 0, -1.0, 1.0)[None, :]).T
    c["ropeC"] = np.ascontiguousarray(np.concatenate([C64, C64], 0)).astype(np.float32)
    c["ropeS"] = np.ascontiguousarray(np.concatenate([S64, S64], 0)).astype(np.float32)
    return c


def host_layout(inputs):
    f = np.float32
    w_in = inputs["w_in"]
    L = w_in.shape[0]
    sp = np.cumsum((256, 256, 512, 512, 16, 16, 512, 128, 128, 1024, 1024))
    q_a = w_in[:, :, 0:sp[0]]; k_a = w_in[:, :, sp[0]:sp[1]]; v_a = w_in[:, :, sp[1]:sp[2]]
    r_a = w_in[:, :, sp[2]:sp[3]]; lr = w_in[:, :, sp[3]:sp[5]]
    q_b = w_in[:, :, sp[5]:sp[6]]; k_b = w_in[:, :, sp[6]:sp[7]]; v_b = w_in[:, :, sp[7]:sp[8]]
    g_a = w_in[:, :, sp[8]:sp[9]]; g_b = w_in[:, :, sp[9]:sp[10]]
    sw = _swap_idx64()
    k_b_sw = k_b.reshape(L, D, 2, 64)[:, :, :, sw].reshape(L, D, 128)
    q4 = q_b.reshape(L, D, 2, 4, 64)
    q_perm = q4.transpose(0, 1, 3, 2, 4).reshape(L, D, 512)
    q_perm_sw = q4[:, :, :, :, sw].transpose(0, 1, 3, 2, 4).reshape(L, D, 512)
    w_ext = np.concatenate([q_a, k_a, v_a, lr, k_b, k_b_sw, v_b,
                            q_a, k_a, v_a, lr, r_a, q_perm, q_perm_sw,
                            g_a, g_b], axis=2).astype(f)
    assert w_ext.shape[2] == WEXT_N
    out = {"w_ext": np.ascontiguousarray(w_ext)}
    out["w_pa"] = np.ascontiguousarray(inputs["w_pa"]).astype(f)
    out["w_pb_r"] = np.ascontiguousarray(inputs["w_pb"].reshape(L, 8, 64, D).transpose(0, 2, 1, 3)).astype(f)
    out["w_out"] = np.ascontiguousarray(inputs["w_out"]).astype(f)
    w2aug = np.zeros((L, 33, 2, 256), f)
    w2aug[:, 0:16, 0, :] = inputs["gla_w2"][:, 0]
    w2aug[:, 16:32, 1, :] = inputs["gla_w2"][:, 1]
    w2aug[:, 32, 0, :] = inputs["gla_b2"][:, 0]
    w2aug[:, 32, 1, :] = inputs["gla_b2"][:, 1]
    out["w2aug"] = w2aug
    cols = np.zeros((L, 128, 32), f)
    qn = inputs["q_norm"]; kn = inputs["k_norm"]
    cols[:, :, 0] = np.concatenate([qn, qn], 1)
    cols[:, :, 1] = np.concatenate([qn[:, sw], qn[:, sw]], 1)
    cols[:, :, 2] = np.concatenate([kn, kn], 1)
    cols[:, :, 3] = np.concatenate([kn[:, sw], kn[:, sw]], 1)
    cols[:, :, 4:8] = inputs["gla_gn"].reshape(L, 4, 128).transpose(0, 2, 1)
    bm = inputs["b_merge"].reshape(L, 2, 8, 128)
    cols[:, :, 8:16] = bm[:, 0].transpose(0, 2, 1)
    cols[:, :, 16:24] = bm[:, 1].transpose(0, 2, 1)
    cols[:, 112:, 24] = 1.0
    out["cols"] = cols
    return out


def make_in_maps(inputs, n_cores, consts):
    lay = host_layout(inputs)
    shared = dict(consts)
    shared.update(lay)
    shared["meta"] = np.ascontiguousarray(inputs["meta_tokens"]).astype(np.float32)
    for k in ("norm_gains", "ffn_w_gate", "ffn_w_up", "ffn_w_down"):
        shared[k] = np.ascontiguousarray(inputs[k]).astype(np.float32)
    shared["final_norm"] = np.ascontiguousarray(inputs["final_norm"]).reshape(1, D).astype(np.float32)
    maps = []
    for c in range(n_cores):
        m = dict(shared)
        m["x"] = np.ascontiguousarray(inputs["x"][c]).astype(np.float32)
        maps.append(m)
    return maps


def kernel(**inputs):
    inputs = {k: np.asarray(v) for k, v in inputs.items()}
    B, SEQ, _ = inputs["x"].shape
    NT = SEQ // 128 + 1
    nc = build_program(NT, all_stages())
    maps = make_in_maps(inputs, B, host_consts(NT))
    res = run_bass_kernel_spmd(nc, maps, core_ids=list(range(B)))
    return np.stack([res.results[c]["y"] for c in range(B)], axis=0).astype(np.float32)
```

```python
import contextlib
import numpy as np
import ml_dtypes
import concourse.bass as bass
import concourse.mybir as mybir
from concourse.bass_utils import run_bass_kernel_spmd

F32 = mybir.dt.float32
BF16 = mybir.dt.bfloat16
AF = mybir.ActivationFunctionType
ALU = mybir.AluOpType

D = 1024
DFF = 2816
NFC = DFF // 128
DEPTH = 4
EPS = 1e-6
DIN = 4384
SAME_ENGINE_SYNC = True


class Rec:
    ENGS = ("pe", "act", "dve", "pool", "sp")

    def __init__(self, nc, es):
        self.nc = nc
        self.es = es
        self.ops = []
        self.last_w = {}
        self.readers = {}
        self.chan_R = {}
        self.chan_n = {}
        self.slot_last = {}
        self.last_sk = {}
        self.bar = {}
        self.bar_pending = set()

    def barrier(self):
        self.bar = dict(self.last_sk)
        self.bar_pending = set(self.ENGS)

    def chan(self, name, R):
        self.chan_R[name] = R
        self.chan_n[name] = 0

    def add(self, eng, fn, reads=(), writes=(), chan=None):
        i = len(self.ops)
        deps = set()
        for b in reads:
            w = self.last_w.get(b)
            if w is not None:
                deps.add(w)
        for b in writes:
            w = self.last_w.get(b)
            if w is not None:
                deps.add(w)
            r = self.readers.get(b)
            if r:
                deps.update(r.values())
        sk = None
        if chan is not None:
            n = self.chan_n[chan]
            self.chan_n[chan] = n + 1
            sk = ("c", chan, n % self.chan_R[chan])
            prev = self.slot_last.get(sk)
            if prev is not None:
                deps.add(prev)
            self.slot_last[sk] = i
        else:
            sk = ("e", eng)
        if eng in self.bar_pending:
            self.bar_pending.discard(eng)
            deps.update(self.bar.values())
        for b in reads:
            self.readers.setdefault(b, {})[sk] = i
        for b in writes:
            self.last_w[b] = i
            self.readers[b] = {}
        self.last_sk[sk] = i
        self.ops.append([eng, fn, deps, sk])
        return i

    def finalize(self, block):
        ops = self.ops
        nops = len(ops)
        ordn = [0] * nops
        cnt = {}
        for i, (eng, fn, deps, sk) in enumerate(ops):
            cnt[sk] = cnt.get(sk, 0) + 1
            ordn[i] = cnt[sk]
        known = {e: {} for e in self.ENGS}
        waits = [None] * nops
        signal = [False] * nops
        for i, (eng, fn, deps, sk) in enumerate(ops):
            need = {}
            K = known[eng]
            for d in deps:
                dsk = ops[d][3]
                if dsk[0] == "e" and dsk[1] == eng:
                    if eng == "pe" or not SAME_ENGINE_SYNC:
                        continue
                if K.get(dsk, 0) >= ordn[d]:
                    continue
                if dsk not in need or ordn[need[dsk]] < ordn[d]:
                    need[dsk] = d
            for dsk, d in need.items():
                K[dsk] = ordn[d]
                signal[d] = True
            waits[i] = list(need.values())
        val = [0] * nops
        run = {}
        for i, (eng, fn, deps, sk) in enumerate(ops):
            if sk[0] == "c":
                signal[i] = True
                run[sk] = run.get(sk, 0) + 16
                val[i] = run[sk]
            elif signal[i]:
                run[sk] = run.get(sk, 0) + 1
                val[i] = run[sk]
        sems = {}
        for sk in cnt:
            sems[sk] = self.es.enter_context(self.nc.semaphore("s_" + "_".join(str(x) for x in sk[1:])))
        self.nsems = len(sems)
        self.maxvals = dict(run)
        final_chan = {sk: v for sk, v in run.items() if sk[0] == "c"}
        per_eng = {e: [] for e in self.ENGS}
        for i, (eng, fn, deps, sk) in enumerate(ops):
            per_eng[eng].append(i)
        self.eng_counts = {e: (len(v), sum(len(waits[i]) for i in v)) for e, v in per_eng.items()}

        def emit(engname, e):
            for i in per_eng[engname]:
                _, fn, _, sk = ops[i]
                for d in waits[i]:
                    e.wait_ge(sems[ops[d][3]], val[d])
                inst = fn(e)
                if signal[i]:
                    inst.then_inc(sems[sk], 16 if sk[0] == "c" else 1)
            if engname == "sp":
                for sk, v in final_chan.items():
                    e.wait_ge(sems[sk], v)

        @block.tensor
        def _(e):
            emit("pe", e)

        @block.scalar
        def _(e):
            emit("act", e)

        @block.vector
        def _(e):
            emit("dve", e)

        @block.gpsimd
        def _(e):
            emit("pool", e)

        @block.sync
        def _(e):
            emit("sp", e)


WF_N = 1440
WB1_N = 2592
WB2_N = 2048
WEXT_N = WF_N + WB1_N + WB2_N
TMP0 = 36864


def build_program(NT, stages, debug_h=False):
    P = NT * 128
    SEQ = P - 128
    nc = bass.Bass("TRN2", target_bir_lowering=False)

    def din(name, shape):
        return nc.dram_tensor(name, shape, F32, kind="ExternalInput").ap()

    x_d = din("x", [SEQ, D])
    meta_d = din("meta", [16, D])
    ng_d = din("norm_gains", [DEPTH, 3, D])
    wg_d = din("ffn_w_gate", [DEPTH, 2, D, DFF])
    wu_d = din("ffn_w_up", [DEPTH, 2, D, DFF])
    wd_d = din("ffn_w_down", [DEPTH, 2, DFF, D])
    fn_d = din("final_norm", [1, D])
    ident_d = din("ident", [128, 128])
    wext_d = din("w_ext", [DEPTH, D, WEXT_N])
    wpa_d = din("w_pa", [DEPTH, 512, D])
    wpb_d = din("w_pb_r", [DEPTH, 64, 8, D])
    wout_d = din("w_out", [DEPTH, D, D])
    w2aug_d = din("w2aug", [DEPTH, 33, 2, 256])
    cols_d = din("cols", [DEPTH, 128, 32])
    masks_d = din("masks", [128, 4, 128])
    bo_d = din("bo", [128, 2, 128])
    ropeC_d = din("ropeC", [128, P])
    ropeS_d = din("ropeS", [128, P])
    y_d = nc.dram_tensor("y", [SEQ, D], F32, kind="ExternalOutput").ap()
    H = nc.dram_tensor("Hs", [P, D], F32, kind="Internal").ap()
    ZT = nc.dram_tensor("ZTs", [128, 8, P], BF16, kind="Internal").ap()
    OF = nc.dram_tensor("OFs", [128, 4, P], F32, kind="Internal").ap()
    ATs = nc.dram_tensor("ATs", [128, 4, P], BF16, kind="Internal").ap()
    BTs = nc.dram_tensor("BTs", [64, 8, P], BF16, kind="Internal").ap()
    if debug_h:
        hdbg = nc.dram_tensor("hdbg", [P, D], F32, kind="ExternalOutput").ap()

    es = contextlib.ExitStack()
    with es:
        def sb(name, shape, dt):
            return es.enter_context(nc.sbuf_tensor(name, shape, dt))

        def ps(name, shape, dt):
            return es.enter_context(nc.psum_tensor(name, shape, dt))

        rec = Rec(nc, es)
        rec.chan("w", 8)
        rec.chan("hl", 4)
        rec.chan("st", 4)
        rec.chan("misc", 4)
        A = rec.add

        arena = sb("arena", [128, 66 * 1024], BF16)
        ident = sb("ident_sb", [128, 128], BF16)
        gtile = sb("gtile", [128, D], F32)
        hbuf = [sb(f"hbuf{i}", [128, D], F32) for i in range(4)]
        xn = [sb(f"xn{i}", [128, D], BF16) for i in range(2)]
        junk = sb("junk", [128, D], BF16)
        stat = sb("stat", [128, 16], F32)
        xnT = sb("xnT", [128, 8, 512], BF16)
        shared = sb("shared", [128, 13312], BF16)
        masks = sb("masks_sb", [128, 4, 128], BF16)
        mask4 = sb("mask4", [128, 2, 4, 128], BF16)
        bo = sb("bo_sb", [128, 2, 128], BF16)
        w2aug = sb("w2aug_sb", [33, 2, 256], BF16)
        cols = sb("cols_sb", [128, 32], F32)
        Sst = sb("Sst", [64, 512], F32)
        Sbf = sb("Sbf", [64, 512], BF16)

        hT = shared[:, 0:NFC * 512].rearrange("p (j n) -> p j n", j=NFC)
        sgv = shared[:, NFC * 512:NFC * 512 + 2048].bitcast(F32)
        sg = [sgv[:, 0:512], sgv[:, 512:1024]]
        KT = shared[:, 0:P]
        VA = shared[:, P:P + NT * 256].rearrange("p (t g a d) -> p t g a d", t=NT, g=2, a=2)
        assert P + NT * 256 <= 13312

        ps_tr = ps("ps_tr", [128, 8, 128], BF16)
        psb = [ps(f"psb{i}", [128, 512], F32) for i in range(7)]
        ps_g = [psb[0], psb[1]]
        ps_u = [psb[2], psb[3]]
        ps_o = [psb[4], psb[5]]

        def PK(k):
            return ("psb", k)

        A("pool", lambda e: e.dma_start(out=ident[:], in_=ident_d[:]), writes=["ident"], chan="w")
        A("pool", lambda e: e.dma_start(out=masks[:], in_=masks_d[:]), writes=["masks"], chan="w")
        A("pool", lambda e: e.dma_start(out=bo[:], in_=bo_d[:]), writes=["bo"], chan="w")
        for dr in range(2):
            for h in range(4):
                A("pool", lambda e, dr=dr, h=h: e.dma_start(out=mask4[:, dr, h, :], in_=masks_d[:, 2 * dr, :]),
                  writes=["mask4"], chan="w")

        def stage_init():
            hb = hbuf[0]
            A("dve", lambda e: e.memset(hb[:], 0.0), writes=[("hbuf", 0)])
            A("sp", lambda e: e.dma_start(out=H[0:112, :], in_=hb[0:112, :]), reads=[("hbuf", 0)],
              writes=[("H", 0)], chan="st")
            A("sp", lambda e: e.dma_start(out=H[112:128, :], in_=meta_d[:]), writes=[("H", 0)], chan="st")
            for t in range(1, NT):
                A("sp", lambda e, t=t: e.dma_start(out=H[t * 128:(t + 1) * 128, :],
                                                    in_=x_d[(t - 1) * 128:t * 128, :]),
                  writes=[("H", t)], chan="st")

        cnt = {"h": 0, "xn": 0, "g": 0, "o": 0, "sg": 0, "pt": 0, "sc": 0}

        def load_gain(src_row_ap):
            A("sp", lambda e: e.dma_start(out=gtile[:], in_=src_row_ap.partition_broadcast(128)),
              writes=["gtile"], chan="misc")

        def next_hbuf():
            k = cnt["h"] % 4
            cnt["h"] += 1
            return hbuf[k], ("hbuf", k)

        def load_h(t):
            hb, hkey = next_hbuf()
            A("sp", lambda e: e.dma_start(out=hb[:], in_=H[t * 128:(t + 1) * 128, :]),
              reads=[("H", t)], writes=[hkey], chan="hl")
            return hb, hkey

        def norm_tile(hb, hkey, out_ap, out_key):
            s0 = stat[:, 0:1]
            s1 = stat[:, 1:2]
            s2 = stat[:, 2:3]
            A("act", lambda e: e.activation(out=junk[:], in_=hb[:], func=AF.Square, accum_out=s0),
              reads=[hkey], writes=["junk", "s0"])
            A("act", lambda e: e.activation(out=s1, in_=s0, func=AF.Ln, scale=1.0 / D, bias=EPS),
              reads=["s0"], writes=["s1"])
            A("act", lambda e: e.activation(out=s2, in_=s1, func=AF.Exp, scale=-0.5),
              reads=["s1"], writes=["s2"])
            A("dve", lambda e: e.scalar_tensor_tensor(out=out_ap, in0=hb[:], scalar=s2, in1=gtile[:],
                                                      op0=ALU.mult, op1=ALU.mult),
              reads=[hkey, "s2", "gtile"], writes=[out_key])

        def norm_transpose(hb, hkey, ti):
            kx = cnt["xn"] % 2
            cnt["xn"] += 1
            xnb = xn[kx]
            norm_tile(hb, hkey, xnb[:], ("xn", kx))
            for kc in range(8):
                A("pe", lambda e, kc=kc: e.transpose(ps_tr[:, kc, :], xnb[:, kc * 128:(kc + 1) * 128], ident[:]),
                  reads=[("xn", kx), "ident"], writes=["ps_tr"])
            A("act", lambda e: e.activation(out=xnT[:, :, ti * 128:(ti + 1) * 128], in_=ps_tr[:], func=AF.Copy),
              reads=["ps_tr"], writes=["xnT"])

        def wload(dst, src, key):
            A("pool", lambda e: e.dma_start(out=dst, in_=src), writes=[key], chan="w")

        def ffn_phase(l, f):
            rec.barrier()
            gi = 0 if f == 0 else 2
            load_gain(ng_d[l, gi:gi + 1, :])
            wgv = wg_d[l, f].rearrange("(kc p) n -> p kc n", p=128)
            wuv = wu_d[l, f].rearrange("(kc p) n -> p kc n", p=128)
            WG = arena[:, 0:22 * 1024].rearrange("p (kc n) -> p kc n", kc=8)
            WU = arena[:, 22 * 1024:44 * 1024].rearrange("p (kc n) -> p kc n", kc=8)
            WD = arena[:, 44 * 1024:66 * 1024].rearrange("p (j n) -> p j n", j=NFC)
            groups = [(0, 512), (512, 512), (1024, 512), (1536, 512), (2048, 512), (2560, 256)]
            for (c0, cw) in groups:
                wload(WG[:, :, c0:c0 + cw], wgv[:, :, c0:c0 + cw], ("wg", c0 // 512))
                wload(WU[:, :, c0:c0 + cw], wuv[:, :, c0:c0 + cw], ("wu", c0 // 512))
            wdv = wd_d[l, f].rearrange("(j p) n -> p j n", p=128)
            for j0 in range(0, NFC, 4):
                j1 = min(NFC, j0 + 4)
                wload(WD[:, j0:j1, :], wdv[:, j0:j1, :], ("wd", j0 // 4))

            for b0 in range(0, NT, 4):
                tiles = list(range(b0, min(NT, b0 + 4)))
                n = len(tiles) * 128
                hbs = {}
                for ti, t in enumerate(tiles):
                    hb, hkey = load_h(t)
                    hbs[t] = (hb, hkey)
                    norm_transpose(hb, hkey, ti)
                for j in range(NFC):
                    kg = cnt["g"] % 2
                    cnt["g"] += 1
                    pg, pu = ps_g[kg], ps_u[kg]
                    for kc in range(8):
                        A("pe", lambda e, pg=pg, kc=kc, j=j, n=n: e.matmul(
                            pg[:, :n], WG[:, kc, j * 128:(j + 1) * 128], xnT[:, kc, :n],
                            start=(kc == 0), stop=(kc == 7)),
                            reads=["xnT", ("wg", j // 4)], writes=[PK(kg)])
                    for kc in range(8):
                        A("pe", lambda e, pu=pu, kc=kc, j=j, n=n: e.matmul(
                            pu[:, :n], WU[:, kc, j * 128:(j + 1) * 128], xnT[:, kc, :n],
                            start=(kc == 0), stop=(kc == 7)),
                            reads=["xnT", ("wu", j // 4)], writes=[PK(2 + kg)])
                    ks = cnt["sg"] % 2
                    cnt["sg"] += 1
                    sgb = sg[ks]
                    A("act", lambda e, sgb=sgb, pg=pg, n=n: e.activation(out=sgb[:, :n], in_=pg[:, :n], func=AF.Silu),
                      reads=[PK(kg)], writes=[("sg", ks)])
                    A("dve", lambda e, sgb=sgb, pu=pu, j=j, n=n: e.tensor_tensor(
                        out=hT[:, j, :n], in0=sgb[:, :n], in1=pu[:, :n], op=ALU.mult),
                        reads=[("sg", ks), PK(2 + kg)], writes=[("hT", j)])
                for ti, t in enumerate(tiles):
                    hb, hkey = hbs[t]
                    for half in range(2):
                        ko = cnt["o"] % 2
                        cnt["o"] += 1
                        po = ps_o[ko]
                        for j in range(NFC):
                            A("pe", lambda e, po=po, j=j, ti=ti, half=half: e.matmul(
                                po[:, :], hT[:, j, ti * 128:(ti + 1) * 128], WD[:, j, half * 512:(half + 1) * 512],
                                start=(j == 0), stop=(j == NFC - 1)),
                                reads=[("hT", j), ("wd", j // 4)], writes=[PK(4 + ko)])
                        A("dve", lambda e, po=po, hb=hb, half=half: e.scalar_tensor_tensor(
                            out=hb[:, half * 512:(half + 1) * 512], in0=po[:, :], scalar=0.5,
                            in1=hb[:, half * 512:(half + 1) * 512], op0=ALU.mult, op1=ALU.add),
                            reads=[PK(4 + ko), hkey], writes=[hkey])
                    A("sp", lambda e, hb=hb, t=t: e.dma_start(out=H[t * 128:(t + 1) * 128, :], in_=hb[:]),
                      reads=[hkey], writes=[("H", t)], chan="st")

        class Carver:
            def __init__(self, start):
                self.o = start

            def bf(self, parts, shape_free):
                n = int(np.prod(shape_free))
                n16 = (n + 15) // 16 * 16
                v = arena[0:parts, self.o:self.o + n]
                self.o += n16
                assert self.o <= 66 * 1024, self.o
                return v

            def f32(self, parts, shape_free):
                n = int(np.prod(shape_free)) * 2
                n16 = (n + 15) // 16 * 16
                v = arena[0:parts, self.o:self.o + n].bitcast(F32)
                self.o += n16
                assert self.o <= 66 * 1024, self.o
                return v

        def mm_proj(out_ap, W, c0, M, zsl, key_w, pkey, extra_reads=()):
            for kc in range(8):
                A("pe", lambda e, kc=kc: e.matmul(out_ap, W[:, kc, c0:c0 + M], xnT[:, kc, zsl],
                                                   start=(kc == 0), stop=(kc == 7)),
                  reads=["xnT", key_w] + list(extra_reads), writes=[pkey])

        def mixer_phase(l):
            NB = 2
            rec.barrier()
            load_gain(ng_d[l, 1:2, :])
            A("sp", lambda e: e.dma_start(out=cols[:], in_=cols_d[l]), writes=["cols"], chan="misc")
            A("pool", lambda e: e.dma_start(out=w2aug[:], in_=w2aug_d[l]), writes=["w2aug"], chan="w")
            wv = wext_d[l].rearrange("(kc p) n -> p kc n", p=128)
            WF = arena[:, 0:8 * WF_N].rearrange("p (kc n) -> p kc n", kc=8)
            wload(WF[:, :, 0:528], wv[:, :, 0:528], "WFa")
            wload(WF[:, :, 528:1056], wv[:, :, 528:1056], "WFa2")
            wload(WF[:, :, 1056:1440], wv[:, :, 1056:1440], "WFb")
            cv = Carver(TMP0)
            T = {}
            T["qTa"] = cv.bf(64, [512]); T["kTa"] = cv.bf(64, [512]); T["lrT"] = cv.bf(33, [128])
            T["la"] = cv.bf(128, [256]); T["kend"] = cv.bf(128, [256]); T["va"] = cv.bf(128, [512])
            T["qdec"] = cv.bf(64, [512]); T["kinv"] = cv.bf(64, [512]); T["attT"] = cv.bf(128, [512])
            T["sqb"] = cv.bf(128, [256])
            T["aT"] = cv.bf(128, [4 * 256]).rearrange("p (h n) -> p h n", h=4)
            T["BT"] = cv.bf(64, [8 * 256]).rearrange("p (h n) -> p h n", h=8)
            T["QT"] = cv.bf(128, [4 * 256]).rearrange("p (j n) -> p j n", j=4)
            T["PT"] = [cv.bf(128, [512]) for _ in range(3)]
            T["e1"] = cv.f32(128, [256]); T["ed"] = cv.f32(64, [512]); T["ei"] = cv.f32(64, [512])
            T["er"] = cv.f32(128, [256])
            T["ob"] = cv.f32(128, [4 * 256]).rearrange("p (h n) -> p h n", h=4)
            T["oft"] = cv.f32(128, [512])
            T["t1"] = cv.f32(128, [256]); T["t2"] = cv.f32(128, [256])
            T["lnt"] = cv.f32(128, [256]); T["rst"] = cv.f32(128, [256])
            T["rc"] = cv.f32(64, [512]); T["sr"] = cv.f32(128, [256])
            T["rC"] = cv.f32(128, [256]); T["rS"] = cv.f32(128, [256])

            A("dve", lambda e: e.memset(T["lrT"][32:33, :], 1.0), writes=["lrT"])
            A("dve", lambda e: e.memset(VA[:, :, :, 1, :], 1.0), writes=["VA"])
            A("dve", lambda e: e.memset(VA[0:112, 0, :, 1, :], 0.0), writes=["VA"])

            def rope_norm(psq, kq, psqs, kqs, pss, kss, n, g0, g1, out_ap, out_key):
                A("act", lambda e: e.activation(out=T["sqb"][:, :n], in_=psq[:, :n], func=AF.Square),
                  reads=[kq], writes=["sqb"])
                A("pe", lambda e: e.matmul(pss[:, :n], bo[:, 0, :], T["sqb"][:, :n], start=True, stop=True),
                  reads=["sqb", "bo"], writes=[kss])
                A("act", lambda e: e.activation(out=T["lnt"][:, :n], in_=pss[:, :n], func=AF.Ln, scale=1.0 / 64, bias=EPS),
                  reads=[kss], writes=["lnt"])
                A("act", lambda e: e.activation(out=T["rst"][:, :n], in_=T["lnt"][:, :n], func=AF.Exp, scale=-0.5),
                  reads=["lnt"], writes=["rst"])
                A("dve", lambda e: e.scalar_tensor_tensor(out=T["t1"][:, :n], in0=psq[:, :n], scalar=g0, in1=T["rC"][:, :n],
                                                          op0=ALU.mult, op1=ALU.mult),
                  reads=[kq, "rope", "cols"], writes=["t1"])
                A("dve", lambda e: e.scalar_tensor_tensor(out=T["t2"][:, :n], in0=psqs[:, :n], scalar=g1, in1=T["rS"][:, :n],
                                                          op0=ALU.mult, op1=ALU.mult),
                  reads=[kqs, "rope", "cols"], writes=["t2"])
                A("dve", lambda e: e.tensor_tensor(out=T["t1"][:, :n], in0=T["t1"][:, :n], in1=T["t2"][:, :n], op=ALU.add),
                  reads=["t1", "t2"], writes=["t1"])
                A("dve", lambda e: e.tensor_tensor(out=out_ap, in0=T["t1"][:, :n], in1=T["rst"][:, :n], op=ALU.mult),
                  reads=["t1", "rst"], writes=[out_key])

            def load_rope(p0, n):
                A("sp", lambda e: e.dma_start(out=T["rC"][:, :n], in_=ropeC_d[:, p0:p0 + n]), writes=["rope"], chan="misc")
                A("sp", lambda e: e.dma_start(out=T["rS"][:, :n], in_=ropeS_d[:, p0:p0 + n]), writes=["rope"], chan="misc")

            def gla_tile(dr, ti, W, wkeys):
                zs = slice(ti * 128, (ti + 1) * 128)
                b0, b1, b2, b3, b4, b5, b6 = psb
                for h in range(4):
                    mm_proj(b0[0:64, h * 128:(h + 1) * 128], W, h * 64, 64, zs, wkeys[0], PK(0))
                for h in range(4):
                    mm_proj(b1[0:64, h * 128:(h + 1) * 128], W, 256 + h * 64, 64, zs, wkeys[0], PK(1))
                mm_proj(b2[0:32, 0:128], W, 1024, 32, zs, wkeys[1], PK(2))
                for kc in range(8):
                    A("pe", lambda e, kc=kc: e.matmul(b3[:, 0:512], xnT[:, kc, zs], W[:, kc, 256:768],
                                                       start=(kc == 0), stop=(kc == 7)),
                      reads=["xnT", wkeys[0], wkeys[1]], writes=[PK(3)])
                for kc in range(8):
                    A("pe", lambda e, kc=kc: e.matmul(b4[:, 0:256], xnT[:, kc, zs], W[:, kc, 768:1024],
                                                       start=(kc == 0), stop=(kc == 7)),
                      reads=["xnT", wkeys[1]], writes=[PK(4)])
                A("act", lambda e: e.activation(out=T["qTa"][:, :], in_=b0[0:64, :], func=AF.Copy, scale=0.125),
                  reads=[PK(0)], writes=["qTa"])
                A("act", lambda e: e.activation(out=T["kTa"][:, :], in_=b1[0:64, :], func=AF.Copy),
                  reads=[PK(1)], writes=["kTa"])
                A("act", lambda e: e.activation(out=T["lrT"][0:32, :], in_=b2[0:32, 0:128], func=AF.Copy),
                  reads=[PK(2)], writes=["lrT"])
                A("dve", lambda e: e.tensor_copy(out=T["va"][:, 0:256], in_=b3[:, 256:512]), reads=[PK(3)], writes=["va"])
                A("dve", lambda e: e.tensor_copy(out=T["va"][:, 256:512], in_=b4[:, 0:256]), reads=[PK(4)], writes=["va"])
                A("pe", lambda e: e.matmul(b2[:, 256:512], T["lrT"][0:33, :], w2aug[0:33, dr, :], start=True, stop=True),
                  reads=["lrT", "w2aug"], writes=[PK(2)])
                A("act", lambda e: e.activation(out=T["e1"][:, :], in_=b2[:, 256:512], func=AF.Exp, scale=-1.0),
                  reads=[PK(2)], writes=["e1"])
                A("act", lambda e: e.activation(out=T["la"][:, :], in_=T["e1"][:, :], func=AF.Ln, bias=1.0),
                  reads=["e1"], writes=["la"])
                for h in range(4):
                    A("pe", lambda e, h=h: e.matmul(b5[0:64, h * 128:(h + 1) * 128], T["la"][:, h * 64:(h + 1) * 64],
                                                     masks[:, 2 * dr, :], start=True, stop=True),
                      reads=["la", "masks"], writes=[PK(5)])
                A("pe", lambda e: e.matmul(b4[:, 256:512], masks[:, 2 * dr + 1, :], T["la"][:, :], start=True, stop=True),
                  reads=["la", "masks"], writes=[PK(4)])
                A("act", lambda e: e.activation(out=T["ed"][:, :], in_=b5[0:64, :], func=AF.Exp, scale=-1.0 / 16),
                  reads=[PK(5)], writes=["ed"])
                A("act", lambda e: e.activation(out=T["ei"][:, :], in_=b5[0:64, :], func=AF.Exp, scale=1.0 / 16),
                  reads=[PK(5)], writes=["ei"])
                A("act", lambda e: e.activation(out=T["er"][:, :], in_=b4[:, 256:512], func=AF.Exp, scale=-1.0 / 16),
                  reads=[PK(4)], writes=["er"])
                A("dve", lambda e: e.tensor_tensor(out=T["qdec"][:, :], in0=T["qTa"][:, :], in1=T["ed"][:, :], op=ALU.mult),
                  reads=["qTa", "ed"], writes=["qdec"])
                A("dve", lambda e: e.tensor_tensor(out=T["kinv"][:, :], in0=T["kTa"][:, :], in1=T["ei"][:, :], op=ALU.mult),
                  reads=["kTa", "ei"], writes=["kinv"])
                A("dve", lambda e: e.tensor_tensor(out=T["kend"][:, :], in0=b3[:, 0:256], in1=T["er"][:, :], op=ALU.mult),
                  reads=[PK(3), "er"], writes=["kend"])
                for h in range(4):
                    A("pe", lambda e, h=h: e.matmul(b6[:, h * 128:(h + 1) * 128], T["kinv"][:, h * 128:(h + 1) * 128],
                                                     T["qdec"][:, h * 128:(h + 1) * 128], start=True, stop=True),
                      reads=["kinv", "qdec"], writes=[PK(6)])
                A("dve", lambda e: e.tensor_tensor(out=T["attT"][:, :], in0=b6[:, :],
                                                   in1=mask4[:, dr].rearrange("p h i -> p (h i)"), op=ALU.mult),
                  reads=[PK(6), "mask4"], writes=["attT"])
                order = (0, 1) if dr == 0 else (1, 0)
                edv = T["ed"].rearrange("p (h i) -> p h i", h=4)
                Sv = Sst[:, :].rearrange("p (h v) -> p h v", h=4)
                for ch in order:
                    cs = slice(ch * 64, (ch + 1) * 64)
                    for h in range(4):
                        osl = slice(h * 128 + ch * 64, h * 128 + ch * 64 + 64)
                        A("pe", lambda e, h=h, osl=osl: e.matmul(b0[:, osl], T["va"][:, h * 128:(h + 1) * 128],
                                                                   T["attT"][:, osl], start=True, stop=False),
                          reads=["va", "attT"], writes=[PK(0)])
                        A("pe", lambda e, h=h, osl=osl: e.matmul(b0[:, osl], Sbf[:, h * 128:(h + 1) * 128],
                                                                   T["qdec"][:, osl], start=False, stop=True),
                          reads=["Sbf", "qdec"], writes=[PK(0)])
                    for h in range(4):
                        A("pe", lambda e, h=h, cs=cs: e.matmul(b1[0:64, h * 128:(h + 1) * 128],
                                                                 T["kend"][cs, h * 64:(h + 1) * 64],
                                                                 T["va"][cs, h * 128:(h + 1) * 128], start=True, stop=True),
                          reads=["kend", "va"], writes=[PK(1)])
                    col = ch * 64 + 63 if dr == 0 else ch * 64
                    A("dve", lambda e, col=col: e.tensor_tensor(out=Sv, in0=Sv,
                                                                 in1=edv[:, :, col:col + 1].to_broadcast([64, 4, 128]),
                                                                 op=ALU.mult),
                      reads=["S", "ed"], writes=["S"])
                    A("dve", lambda e: e.tensor_tensor(out=Sst[:, :], in0=Sst[:, :], in1=b1[0:64, :], op=ALU.add),
                      reads=["S", PK(1)], writes=["S"])
                    A("act", lambda e: e.activation(out=Sbf[:, :], in_=Sst[:, :], func=AF.Copy),
                      reads=["S"], writes=["Sbf"])

            def reset_state():
                A("dve", lambda e: e.memset(Sst[:, :], 0.0), writes=["S"])
                A("dve", lambda e: e.memset(Sbf[:, :], 0.0), writes=["Sbf"])

            reset_state()
            for b0_ in range(0, NT, NB):
                tiles = list(range(b0_, min(NT, b0_ + NB)))
                n = len(tiles) * 128
                p0 = b0_ * 128
                for ti, t in enumerate(tiles):
                    hb, hkey = load_h(t)
                    norm_transpose(hb, hkey, ti)
                A("sp", lambda e, n=n, p0=p0: e.dma_start(out=ZT[:, :, p0:p0 + n], in_=xnT[:, :, :n]),
                  reads=["xnT"], writes=[("ZT", b0_)], chan="st")
                load_rope(p0, n)
                mm_proj(psb[3][:, :n], WF, 1056, 128, slice(0, n), "WFb", PK(3))
                mm_proj(psb[4][:, :n], WF, 1184, 128, slice(0, n), "WFb", PK(4))
                rope_norm(psb[3], PK(3), psb[4], PK(4), psb[5], PK(5), n, cols[:, 2:3], cols[:, 3:4],
                          KT[:, p0:p0 + n], "KT")
                for ti, t in enumerate(tiles):
                    zs = slice(ti * 128, (ti + 1) * 128)
                    for kc in range(8):
                        A("pe", lambda e, kc=kc, zs=zs: e.matmul(psb[6][:, 0:128], xnT[:, kc, zs], WF[:, kc, 1312:1440],
                                                                   start=(kc == 0), stop=(kc == 7)),
                          reads=["xnT", "WFb"], writes=[PK(6)])
                    A("act", lambda e, t=t: e.activation(out=VA[:, t, :, 0, :],
                                                         in_=psb[6][:, 0:128].rearrange("p (g d) -> p g d", g=2),
                                                         func=AF.Copy),
                      reads=[PK(6)], writes=["VA"])
                for ti, t in enumerate(tiles):
                    gla_tile(0, ti, WF, ("WFa", "WFa2"))
                    A("act", lambda e: e.activation(out=T["oft"][:, :], in_=psb[0][:, :], func=AF.Copy),
                      reads=[PK(0)], writes=["oft"])
                    A("sp", lambda e, t=t: e.dma_start(out=OF[:, :, t * 128:(t + 1) * 128],
                                                        in_=T["oft"].rearrange("p (h i) -> p h i", h=4)),
                      reads=["oft"], writes=[("OF", t)], chan="st")

            rec.barrier()
            WB1 = arena[:, 0:8 * WB1_N].rearrange("p (kc n) -> p kc n", kc=8)
            o1 = WF_N
            wload(WB1[:, :, 0:528], wv[:, :, o1:o1 + 528], "WBa")
            wload(WB1[:, :, 528:1056], wv[:, :, o1 + 528:o1 + 1056], "WBa2")
            wload(WB1[:, :, 1056:1568], wv[:, :, o1 + 1056:o1 + 1568], "WBr")
            wload(WB1[:, :, 1568:2080], wv[:, :, o1 + 1568:o1 + 2080], "WBq")
            wload(WB1[:, :, 2080:2592], wv[:, :, o1 + 2080:o1 + 2592], "WBqs")
            reset_state()
            blocks = list(range(0, NT, NB))
            for b0_ in reversed(blocks):
                tiles = list(range(b0_, min(NT, b0_ + NB)))
                n = len(tiles) * 128
                p0 = b0_ * 128
                A("sp", lambda e, n=n, p0=p0: e.dma_start(out=xnT[:, :, :n], in_=ZT[:, :, p0:p0 + n]),
                  reads=[("ZT", b0_)], writes=["xnT"], chan="hl")
                load_rope(p0, n)
                for ti in reversed(range(len(tiles))):
                    t = tiles[ti]
                    A("sp", lambda e, t=t: e.dma_start(out=T["oft"].rearrange("p (h i) -> p h i", h=4),
                                                        in_=OF[:, :, t * 128:(t + 1) * 128]),
                      reads=[("OF", t)], writes=["oft"], chan="hl")
                    gla_tile(1, ti, WB1, ("WBa", "WBa2"))
                    A("dve", lambda e, ti=ti: e.tensor_tensor(out=T["ob"][:, :, ti * 128:(ti + 1) * 128],
                                                               in0=psb[0][:, :].rearrange("p (h i) -> p h i", h=4),
                                                               in1=T["oft"].rearrange("p (h i) -> p h i", h=4), op=ALU.add),
                      reads=[PK(0), "oft"], writes=["ob"])
                for h in range(4):
                    A("act", lambda e, h=h, n=n: e.activation(out=T["sqb"][:, :n], in_=T["ob"][:, h, :n], func=AF.Square),
                      reads=["ob"], writes=["sqb"])
                    A("pe", lambda e, n=n: e.matmul(psb[3][:, :n], bo[:, 1, :], T["sqb"][:, :n], start=True, stop=True),
                      reads=["sqb", "bo"], writes=[PK(3)])
                    A("act", lambda e, n=n: e.activation(out=T["lnt"][:, :n], in_=psb[3][:, :n], func=AF.Ln,
                                                         scale=1.0 / 128, bias=EPS),
                      reads=[PK(3)], writes=["lnt"])
                    A("act", lambda e, n=n: e.activation(out=T["rst"][:, :n], in_=T["lnt"][:, :n], func=AF.Exp, scale=-0.5),
                      reads=["lnt"], writes=["rst"])
                    mm_proj(psb[4][:, :n], WB1, 1056 + h * 128, 128, slice(0, n), "WBr", PK(4))
                    A("act", lambda e, n=n: e.activation(out=T["sr"][:, :n], in_=psb[4][:, :n], func=AF.Silu),
                      reads=[PK(4)], writes=["sr"])
                    A("dve", lambda e, h=h, n=n: e.scalar_tensor_tensor(out=T["t1"][:, :n], in0=T["ob"][:, h, :n],
                                                                         scalar=cols[:, 4 + h:5 + h], in1=T["rst"][:, :n],
                                                                         op0=ALU.mult, op1=ALU.mult),
                      reads=["ob", "rst", "cols"], writes=["t1"])
                    A("dve", lambda e, h=h, n=n: e.tensor_tensor(out=T["aT"][:, h, :n], in0=T["t1"][:, :n],
                                                                  in1=T["sr"][:, :n], op=ALU.mult),
                      reads=["t1", "sr"], writes=["aT"])
                A("sp", lambda e, n=n, p0=p0: e.dma_start(out=ATs[:, :, p0:p0 + n], in_=T["aT"][:, :, :n]),
                  reads=["aT"], writes=[("ATs", b0_)], chan="st")
                for j in range(4):
                    mm_proj(psb[3][:, :n], WB1, 1568 + j * 128, 128, slice(0, n), "WBq", PK(3))
                    mm_proj(psb[4][:, :n], WB1, 2080 + j * 128, 128, slice(0, n), "WBqs", PK(4))
                    rope_norm(psb[3], PK(3), psb[4], PK(4), psb[5], PK(5), n, cols[:, 0:1], cols[:, 1:2],
                              T["QT"][:, j, :n], "QT")
                for ti, t in enumerate(tiles):
                    qs = slice(ti * 128, (ti + 1) * 128)
                    for sblk in range(NT):
                        for g in range(2):
                            ksc = cnt["sc"] % 4
                            cnt["sc"] += 1
                            sc = psb[ksc]
                            A("pe", lambda e, sc=sc, g=g, sblk=sblk, qs=qs: e.matmul(
                                sc[:, :], KT[g * 64:(g + 1) * 64, sblk * 128:(sblk + 1) * 128],
                                T["QT"][g * 64:(g + 1) * 64, :, qs], start=True, stop=True),
                                reads=["KT", "QT"], writes=[PK(ksc)])
                            kp = cnt["pt"] % 3
                            cnt["pt"] += 1
                            pt = T["PT"][kp]
                            A("act", lambda e, sc=sc, pt=pt: e.activation(out=pt[:, :], in_=sc[:, :], func=AF.Exp, scale=0.125),
                              reads=[PK(ksc)], writes=[("PT", kp)])
                            A("pe", lambda e, pt=pt, g=g, sblk=sblk: e.matmul(
                                psb[4 + g][:, :], VA[:, sblk, g, :, :].rearrange("p a d -> p (a d)"), pt[:, :],
                                start=(sblk == 0), stop=(sblk == NT - 1)),
                                reads=[("PT", kp), "VA"], writes=[PK(4 + g)])
                    for g in range(2):
                        A("dve", lambda e, g=g: e.reciprocal(out=T["rc"][:, :], in_=psb[4 + g][64:128, :]),
                          reads=[PK(4 + g)], writes=["rc"])
                        A("dve", lambda e, g=g, qs=qs: e.tensor_tensor(
                            out=T["BT"][:, g * 4:(g + 1) * 4, qs],
                            in0=psb[4 + g][0:64, :].rearrange("p (j q) -> p j q", j=4),
                            in1=T["rc"][:, :].rearrange("p (j q) -> p j q", j=4), op=ALU.mult),
                            reads=[PK(4 + g), "rc"], writes=["BT"])
                A("sp", lambda e, n=n, p0=p0: e.dma_start(out=BTs[:, :, p0:p0 + n], in_=T["BT"][:, :, :n]),
                  reads=["BT"], writes=[("BTs", b0_)], chan="st")

            rec.barrier()
            o2 = WF_N + WB1_N
            WG2 = arena[:, 0:16384].rearrange("p (kc n) -> p kc n", kc=8)
            WPA = arena[:, 16384:20480].rearrange("p (kc n) -> p kc n", kc=4)
            WPB = arena[0:64, 20480:28672].rearrange("p (h n) -> p h n", h=8)
            WOUT = arena[:, 28672:36864].rearrange("p (kc n) -> p kc n", kc=8)
            wload(WG2[:, :, 0:1024], wv[:, :, o2:o2 + 1024], "WGa")
            wload(WG2[:, :, 1024:2048], wv[:, :, o2 + 1024:o2 + 2048], "WGb")
            wload(WPA, wpa_d[l].rearrange("(kc p) n -> p kc n", p=128), "WPA")
            wload(WPB, wpb_d[l], "WPB")
            wload(WOUT, wout_d[l].rearrange("(kc p) n -> p kc n", p=128), "WOUT")
            cv2 = Carver(TMP0)
            aTl = cv2.bf(128, [4 * 512]).rearrange("p (h n) -> p h n", h=4)
            BTl = cv2.bf(64, [8 * 512]).rearrange("p (h n) -> p h n", h=8)
            yT = cv2.bf(128, [8 * 512]).rearrange("p (c n) -> p c n", c=8)
            sga = cv2.f32(128, [512]); sgb_ = cv2.f32(128, [512])
            u1 = cv2.f32(128, [512]); u2 = cv2.f32(128, [512])
            for b0_ in range(0, NT, 4):
                tiles = list(range(b0_, min(NT, b0_ + 4)))
                n = len(tiles) * 128
                p0 = b0_ * 128
                A("sp", lambda e, n=n, p0=p0: e.dma_start(out=xnT[:, :, :n], in_=ZT[:, :, p0:p0 + n]),
                  writes=["xnT"], chan="hl")
                A("sp", lambda e, n=n, p0=p0: e.dma_start(out=aTl[:, :, :n], in_=ATs[:, :, p0:p0 + n]),
                  writes=["aTl"], chan="hl")
                A("sp", lambda e, n=n, p0=p0: e.dma_start(out=BTl[:, :, :n], in_=BTs[:, :, p0:p0 + n]),
                  writes=["BTl"], chan="hl")
                for c in range(8):
                    mm_proj(psb[0][:, :n], WG2, c * 128, 128, slice(0, n), "WGa", PK(0))
                    mm_proj(psb[1][:, :n], WG2, 1024 + c * 128, 128, slice(0, n), "WGb", PK(1))
                    A("act", lambda e, c=c, n=n: e.activation(out=sga[:, :n], in_=psb[0][:, :n], func=AF.Sigmoid,
                                                              bias=cols[:, 8 + c:9 + c]),
                      reads=[PK(0), "cols"], writes=["sga"])
                    A("act", lambda e, c=c, n=n: e.activation(out=sgb_[:, :n], in_=psb[1][:, :n], func=AF.Sigmoid,
                                                              bias=cols[:, 16 + c:17 + c]),
                      reads=[PK(1), "cols"], writes=["sgb"])
                    for kc in range(4):
                        A("pe", lambda e, kc=kc, c=c, n=n: e.matmul(psb[2][:, :n], WPA[:, kc, c * 128:(c + 1) * 128],
                                                                     aTl[:, kc, :n], start=(kc == 0), stop=(kc == 3)),
                          reads=["aTl", "WPA"], writes=[PK(2)])
                    for h in range(8):
                        A("pe", lambda e, h=h, c=c, n=n: e.matmul(psb[3][:, :n], WPB[:, h, c * 128:(c + 1) * 128],
                                                                   BTl[:, h, :n], start=(h == 0), stop=(h == 7)),
                          reads=["BTl", "WPB"], writes=[PK(3)])
                    A("dve", lambda e, n=n: e.tensor_tensor(out=u1[:, :n], in0=sga[:, :n], in1=psb[2][:, :n], op=ALU.mult),
                      reads=["sga", PK(2)], writes=["u1"])
                    A("dve", lambda e, n=n: e.tensor_tensor(out=u2[:, :n], in0=sgb_[:, :n], in1=psb[3][:, :n], op=ALU.mult),
                      reads=["sgb", PK(3)], writes=["u2"])
                    A("dve", lambda e, c=c, n=n: e.tensor_tensor(out=yT[:, c, :n], in0=u1[:, :n], in1=u2[:, :n], op=ALU.add),
                      reads=["u1", "u2"], writes=["yT"])
                for ti, t in enumerate(tiles):
                    hb, hkey = load_h(t)
                    for half in range(2):
                        ko = cnt["o"] % 2
                        cnt["o"] += 1
                        po = psb[4 + ko]
                        for kc in range(8):
                            A("pe", lambda e, po=po, kc=kc, ti=ti, half=half: e.matmul(
                                po[:, :], yT[:, kc, ti * 128:(ti + 1) * 128], WOUT[:, kc, half * 512:(half + 1) * 512],
                                start=(kc == 0), stop=(kc == 7)),
                                reads=["yT", "WOUT"], writes=[PK(4 + ko)])
                        A("dve", lambda e, po=po, hb=hb, half=half: e.tensor_tensor(
                            out=hb[:, half * 512:(half + 1) * 512], in0=po[:, :],
                            in1=hb[:, half * 512:(half + 1) * 512], op=ALU.add),
                            reads=[PK(4 + ko), hkey], writes=[hkey])
                    if t == 0:
                        A("dve", lambda e, hb=hb: e.tensor_scalar(out=hb[:], in0=hb[:], scalar1=cols[:, 24:25],
                                                                   scalar2=None, op0=ALU.mult),
                          reads=[hkey, "cols"], writes=[hkey])
                    A("sp", lambda e, hb=hb, t=t: e.dma_start(out=H[t * 128:(t + 1) * 128, :], in_=hb[:]),
                      reads=[hkey], writes=[("H", t)], chan="st")

        def final_phase():
            rec.barrier()
            load_gain(fn_d[0:1, :])
            for t in range(1, NT):
                hb, hkey = load_h(t)
                norm_tile(hb, hkey, hb[:], hkey)
                A("sp", lambda e, hb=hb, t=t: e.dma_start(out=y_d[(t - 1) * 128:t * 128, :], in_=hb[:]),
                  reads=[hkey], writes=[("Y", t)], chan="st")

        def dump_h():
            rec.barrier()
            for t in range(NT):
                hb, hkey = load_h(t)
                A("sp", lambda e, hb=hb, t=t: e.dma_start(out=hdbg[t * 128:(t + 1) * 128, :], in_=hb[:]),
                  reads=[hkey], writes=[("HD", t)], chan="st")

        for st in stages:
            if st[0] == "init":
                stage_init()
            elif st[0] == "ffn":
                ffn_phase(st[1], st[2])
            elif st[0] == "mixer":
                mixer_phase(st[1])
            elif st[0] == "final":
                final_phase()
        import os as _os
        for _i in range(int(_os.environ.get("DUMMY_ACT", "0"))):
            A("act", lambda e: e.activation(out=junk[:, 0:64], in_=junk[:, 64:128], func=AF.Copy), writes=["junkx"])
        for _i in range(int(_os.environ.get("DUMMY_DVE", "0"))):
            A("dve", lambda e: e.tensor_copy(out=junk[:, 128:192], in_=junk[:, 192:256]), writes=["junky"])
        for _i in range(int(_os.environ.get("DUMMY_SP", "0"))):
            A("sp", lambda e: e.dma_start(out=hbuf[0][:], in_=H[0:128, :]), writes=[("hbuf", 0)], chan="hl")
        if debug_h:
            dump_h()

        block = es.enter_context(nc.Block())
        rec.finalize(block)
        print("ops", len(rec.ops), "sems", rec.nsems, "maxsem", max(rec.maxvals.values()), rec.eng_counts, flush=True)
    return nc


def all_stages():
    st = [("init",)]
    for l in range(DEPTH):
        st.append(("ffn", l, 0))
        st.append(("mixer", l))
        st.append(("ffn", l, 1))
    st.append(("final",))
    return st


def _swap_idx64():
    d = np.arange(64)
    a, b, i = d // 32, (d % 32) // 16, d % 16
    return a * 32 + (1 - b) * 16 + i


def host_consts(NT):
    P = NT * 128
    c = {}
    c["ident"] = np.eye(128, dtype=np.float32)
    j = np.arange(128)[:, None]
    i = np.arange(128)[None, :]
    same = (j // 64) == (i // 64)
    m = np.zeros((128, 4, 128), np.float32)
    m[:, 0] = (same & (j <= i))
    m[:, 1] = (same & (j > i))
    m[:, 2] = (same & (j >= i))
    m[:, 3] = (same & (j < i))
    c["masks"] = m
    bo = np.zeros((128, 2, 128), np.float32)
    bo[:, 0] = same
    bo[:, 1] = 1.0
    c["bo"] = bo
    pos = np.arange(P)
    n = np.maximum(pos - 128, 0)
    row = (n // 64).astype(np.float32)
    colp = (n % 64).astype(np.float32)
    inv = (np.float32(10000.0) ** (-(np.arange(0, 32, 2, dtype=np.float32)) / np.float32(32))).astype(np.float32)
    ang = np.stack([row[:, None] * inv[None, :], colp[:, None] * inv[None, :]], axis=1).astype(np.float32)
    ang[pos < 128] = 0.0
    cs = np.cos(ang).astype(np.float32)
    sn = np.sin(ang).astype(np.float32)
    d = np.arange(64)
    a, b, ii = d // 32, (d % 32) // 16, d % 16
    C64 = cs[:, a, ii].T
    S64 = (sn[:, a, ii] * np.where(b == 0, -1.0, 1.0)[None, :]).T
    c["ropeC"] = np.ascontiguousarray(np.concatenate([C64, C64], 0)).astype(np.float32)
    c["ropeS"] = np.ascontiguousarray(np.concatenate([S64, S64], 0)).astype(np.float32)
    return c


def host_layout(inputs):
    f = np.float32
    w_in = inputs["w_in"]
    L = w_in.shape[0]
    sp = np.cumsum((256, 256, 512, 512, 16, 16, 512, 128, 128, 1024, 1024))
    q_a = w_in[:, :, 0:sp[0]]; k_a = w_in[:, :, sp[0]:sp[1]]; v_a = w_in[:, :, sp[1]:sp[2]]
    r_a = w_in[:, :, sp[2]:sp[3]]; lr = w_in[:, :, sp[3]:sp[5]]
    q_b = w_in[:, :, sp[5]:sp[6]]; k_b = w_in[:, :, sp[6]:sp[7]]; v_b = w_in[:, :, sp[7]:sp[8]]
    g_a = w_in[:, :, sp[8]:sp[9]]; g_b = w_in[:, :, sp[9]:sp[10]]
    sw = _swap_idx64()
    k_b_sw = k_b.reshape(L, D, 2, 64)[:, :, :, sw].reshape(L, D, 128)
    q4 = q_b.reshape(L, D, 2, 4, 64)
    q_perm = q4.transpose(0, 1, 3, 2, 4).reshape(L, D, 512)
    q_perm_sw = q4[:, :, :, :, sw].transpose(0, 1, 3, 2, 4).reshape(L, D, 512)
    w_ext = np.concatenate([q_a, k_a, v_a, lr, k_b, k_b_sw, v_b,
                            q_a, k_a, v_a, lr, r_a, q_perm, q_perm_sw,
                            g_a, g_b], axis=2).astype(f)
    assert w_ext.shape[2] == WEXT_N
    out = {"w_ext": np.ascontiguousarray(w_ext)}
    out["w_pa"] = np.ascontiguousarray(inputs["w_pa"]).astype(f)
    out["w_pb_r"] = np.ascontiguousarray(inputs["w_pb"].reshape(L, 8, 64, D).transpose(0, 2, 1, 3)).astype(f)
    out["w_out"] = np.ascontiguousarray(inputs["w_out"]).astype(f)
    w2aug = np.zeros((L, 33, 2, 256), f)
    w2aug[:, 0:16, 0, :] = inputs["gla_w2"][:, 0]
    w2aug[:, 16:32, 1, :] = inputs["gla_w2"][:, 1]
    w2aug[:, 32, 0, :] = inputs["gla_b2"][:, 0]
    w2aug[:, 32, 1, :] = inputs["gla_b2"][:, 1]
    out["w2aug"] = w2aug
    cols = np.zeros((L, 128, 32), f)
    qn = inputs["q_norm"]; kn = inputs["k_norm"]
    cols[:, :, 0] = np.concatenate([qn, qn], 1)
    cols[:, :, 1] = np.concatenate([qn[:, sw], qn[:, sw]], 1)
    cols[:, :, 2] = np.concatenate([kn, kn], 1)
    cols[:, :, 3] = np.concatenate([kn[:, sw], kn[:, sw]], 1)
    cols[:, :, 4:8] = inputs["gla_gn"].reshape(L, 4, 128).transpose(0, 2, 1)
    bm = inputs["b_merge"].reshape(L, 2, 8, 128)
    cols[:, :, 8:16] = bm[:, 0].transpose(0, 2, 1)
    cols[:, :, 16:24] = bm[:, 1].transpose(0, 2, 1)
    cols[:, 112:, 24] = 1.0
    out["cols"] = cols
    return out


def make_in_maps(inputs, n_cores, consts):
    lay = host_layout(inputs)
    shared = dict(consts)
    shared.update(lay)
    shared["meta"] = np.ascontiguousarray(inputs["meta_tokens"]).astype(np.float32)
    for k in ("norm_gains", "ffn_w_gate", "ffn_w_up", "ffn_w_down"):
        shared[k] = np.ascontiguousarray(inputs[k]).astype(np.float32)
    shared["final_norm"] = np.ascontiguousarray(inputs["final_norm"]).reshape(1, D).astype(np.float32)
    maps = []
    for c in range(n_cores):
        m = dict(shared)
        m["x"] = np.ascontiguousarray(inputs["x"][c]).astype(np.float32)
        maps.append(m)
    return maps


def kernel(**inputs):
    inputs = {k: np.asarray(v) for k, v in inputs.items()}
    B, SEQ, _ = inputs["x"].shape
    NT = SEQ // 128 + 1
    nc = build_program(NT, all_stages())
    maps = make_in_maps(inputs, B, host_consts(NT))
    res = run_bass_kernel_spmd(nc, maps, core_ids=list(range(B)))
    return np.stack([res.results[c]["y"] for c in range(B)], axis=0).astype(np.float32)
```
